# Optimizing a Trainium2 kernel written in Bass

```python
import math
import jax, jax.numpy as jnp
from jax import lax
import numpy as np

D_MODEL = 2048
BATCH = 4
SEQ = 4096
DEPTH = 2

N_A_LAYERS = (DEPTH + 1) // 2
N_B_LAYERS = DEPTH - N_A_LAYERS

RET_HEADS = 8
RET_DK = D_MODEL // RET_HEADS
RET_DV = 2 * RET_DK
RET_VDIM = RET_HEADS * RET_DV
RET_CHUNK = 128
ROPE_BASE = 10000.0
GN_EPS = 1e-5

ATT_HEADS = 16
ATT_DH = D_MODEL // ATT_HEADS
MOBA_BLOCK = 256
MOBA_TOPK = 3
MOBA_QCHUNK = 16

REL_BUCKETS = 32
REL_MAX_DIST = 1024

D_FF = 4 * D_MODEL
PLE_DIM = 256
RMS_EPS = 1e-6
NEG_INF = -1e30

kernel_name = "yoco_retention_moba_hybrid"


def rms_norm(x, g):
    xf = x.astype(jnp.float32)
    y = xf * lax.rsqrt(jnp.mean(xf * xf, axis=-1, keepdims=True) + RMS_EPS)
    return (y * g.astype(jnp.float32)).astype(x.dtype)


def rotary(x, pos):
    half = x.shape[-1] // 2
    inv = ROPE_BASE ** (-jnp.arange(half, dtype=jnp.float32) / half)
    ang = pos.astype(jnp.float32)[:, None] * inv[None, :]
    cos = jnp.cos(ang)[None, :, None, :]
    sin = jnp.sin(ang)[None, :, None, :]
    x1, x2 = x[..., :half], x[..., half:]
    return jnp.concatenate([x1 * cos - x2 * sin, x1 * sin + x2 * cos], axis=-1)


def retention(xn, w_in, w_out):
    B, S, _ = xn.shape
    C = RET_CHUNK
    nC = S // C
    proj = xn @ w_in
    q, k, v, g = jnp.split(proj, [D_MODEL, 2 * D_MODEL, 2 * D_MODEL + RET_VDIM], axis=-1)
    pos = jnp.arange(S)
    q = rotary(q.reshape(B, S, RET_HEADS, RET_DK).astype(jnp.float32), pos)
    k = rotary(k.reshape(B, S, RET_HEADS, RET_DK).astype(jnp.float32), pos) * (RET_DK ** -0.5)
    v = v.reshape(B, S, RET_HEADS, RET_DV).astype(jnp.float32)

    log_gamma = jnp.log1p(-jnp.exp2(-5.0 - jnp.arange(RET_HEADS, dtype=jnp.float32)))
    idx = jnp.arange(C)
    diff = idx[:, None] - idx[None, :]
    decay_mat = jnp.where(diff[None] >= 0,
                          jnp.exp(jnp.maximum(diff, 0)[None].astype(jnp.float32) * log_gamma[:, None, None]),
                          0.0)
    zeta = jnp.exp((C - 1 - idx)[None].astype(jnp.float32) * log_gamma[:, None])
    xi = jnp.exp((idx + 1)[None].astype(jnp.float32) * log_gamma[:, None])
    chunk_decay = jnp.exp(C * log_gamma)

    def to_chunks(t):
        return t.reshape(B, nC, C, t.shape[2], t.shape[3]).transpose(1, 0, 3, 2, 4)

    qc, kc, vc = to_chunks(q), to_chunks(k), to_chunks(v)
    scores = jnp.einsum('nbhqd,nbhkd->nbhqk', qc, kc) * decay_mat[None, None]
    intra = jnp.einsum('nbhqk,nbhke->nbhqe', scores, vc)

    def step(state, inp):
        qn, kn, vn = inp
        out = jnp.einsum('bhqd,bhde->bhqe', qn, state) * xi[None, :, :, None]
        state = state * chunk_decay[None, :, None, None] + \
            jnp.einsum('bhkd,bhke->bhde', kn * zeta[None, :, :, None], vn)
        return state, out

    state0 = jnp.zeros((B, RET_HEADS, RET_DK, RET_DV), jnp.float32)
    _, inter = lax.scan(step, state0, (qc, kc, vc))
    y = (intra + inter).transpose(1, 0, 3, 2, 4).reshape(B, S, RET_HEADS, RET_DV)
    mu = jnp.mean(y, axis=-1, keepdims=True)
    var = jnp.mean(jnp.square(y - mu), axis=-1, keepdims=True)
    y = ((y - mu) * lax.rsqrt(var + GN_EPS)).reshape(B, S, RET_VDIM)
    out = jax.nn.silu(g.astype(jnp.float32)) * y
    return out.astype(xn.dtype) @ w_out


def t5_bucket(dist):
    n = jnp.maximum(dist, 0)
    max_exact = REL_BUCKETS // 2
    nf = jnp.maximum(n, max_exact).astype(jnp.float32)
    large = max_exact + (jnp.log(nf / max_exact) / math.log(REL_MAX_DIST / max_exact)
                         * (REL_BUCKETS - max_exact)).astype(jnp.int32)
    large = jnp.minimum(large, REL_BUCKETS - 1)
    return jnp.where(n < max_exact, n, large)


def shared_kv(h, g_kv, w_kv):
    B, S, _ = h.shape
    nblk = -(-S // MOBA_BLOCK)
    s_pad = nblk * MOBA_BLOCK
    kv = rms_norm(h, g_kv) @ w_kv
    k, v = jnp.split(kv, 2, axis=-1)

    def blocks(t):
        t = t.reshape(B, S, ATT_HEADS, ATT_DH).transpose(0, 2, 1, 3)
        t = jnp.pad(t, ((0, 0), (0, 0), (0, s_pad - S), (0, 0)))
        return t.reshape(B, ATT_HEADS, nblk, MOBA_BLOCK, ATT_DH)

    k_blocks, v_blocks = blocks(k), blocks(v)
    k_means = jnp.mean(k_blocks.astype(jnp.float32), axis=3)
    return k_blocks, v_blocks, k_means


def moba_attention(xn, w_q, w_o, k_blocks, v_blocks, k_means, rel_bias):
    B, S, _ = xn.shape
    H, dh, BLK, QC = ATT_HEADS, ATT_DH, MOBA_BLOCK, MOBA_QCHUNK
    nblk = k_blocks.shape[2]
    s_pad = nblk * BLK
    nqc = s_pad // QC
    topk = min(MOBA_TOPK, nblk)
    scale = dh ** -0.5

    q = (xn @ w_q).reshape(B, S, H, dh).transpose(0, 2, 1, 3)
    q = jnp.pad(q, ((0, 0), (0, 0), (0, s_pad - S), (0, 0)))
    q_blk = jnp.arange(s_pad) // BLK
    gate = jnp.einsum('bhsd,bhnd->bhsn', q, k_means, preferred_element_type=jnp.float32)
    past = jnp.arange(nblk)[None, :] < q_blk[:, None]
    gate = jnp.where(past[None, None], gate, NEG_INF)
    _, sel = lax.top_k(gate, topk)
    valid = sel < q_blk[None, None, :, None]

    def by_chunk(t):
        t = t.reshape((B, H, nqc, QC) + t.shape[3:])
        t = jnp.moveaxis(t, 2, 1)
        return t.reshape((B * nqc, H, QC) + t.shape[4:])

    qx, selx, validx = by_chunk(q), by_chunk(sel), by_chunk(valid)
    bidx = jnp.repeat(jnp.arange(B, dtype=jnp.int32), nqc)
    cidx = jnp.tile(jnp.arange(nqc, dtype=jnp.int32), B)
    h_ar = jnp.arange(H)
    bias_t = rel_bias.T

    def chunk_attn(args):
        qc, sc, vc, b, c = args
        qpos = c * QC + jnp.arange(QC)
        blk = (c * QC) // BLK
        k_own = lax.dynamic_slice(k_blocks, (b, 0, blk, 0, 0), (1, H, 1, BLK, dh)).reshape(H, BLK, dh)
        v_own = lax.dynamic_slice(v_blocks, (b, 0, blk, 0, 0), (1, H, 1, BLK, dh)).reshape(H, BLK, dh)
        k_sel = k_blocks[b, h_ar[:, None, None], sc]
        v_sel = v_blocks[b, h_ar[:, None, None], sc]

        dist_own = qpos[:, None] - (blk * BLK + jnp.arange(BLK))[None, :]
        l_own = jnp.einsum('hqd,hkd->hqk', qc, k_own, preferred_element_type=jnp.float32) * scale
        l_own = jnp.where(dist_own[None] >= 0, l_own + bias_t[:, t5_bucket(dist_own)], NEG_INF)

        dist_sel = qpos[None, :, None, None] - (sc[..., None] * BLK + jnp.arange(BLK))
        l_sel = jnp.einsum('hqd,hqjkd->hqjk', qc, k_sel, preferred_element_type=jnp.float32) * scale
        l_sel = l_sel + bias_t[h_ar[:, None, None, None], t5_bucket(dist_sel)]
        l_sel = jnp.where(vc[..., None], l_sel, NEG_INF)

        logits = jnp.concatenate([l_own, l_sel.reshape(H, QC, topk * BLK)], axis=-1)
        probs = jax.nn.softmax(logits, axis=-1).astype(v_own.dtype)
        p_own = probs[..., :BLK]
        p_sel = probs[..., BLK:].reshape(H, QC, topk, BLK)
        return jnp.einsum('hqk,hkd->hqd', p_own, v_own) + \
            jnp.einsum('hqjk,hqjkd->hqd', p_sel, v_sel)

    out = lax.map(chunk_attn, (qx, selx, validx, bidx, cidx))
    out = out.reshape(B, nqc, H, QC, dh).transpose(0, 1, 3, 2, 4).reshape(B, s_pad, H * dh)
    return out[:, :S] @ w_o


def sq_relu_mlp(xn, w_up, w_down):
    return jnp.square(jax.nn.relu(xn @ w_up)) @ w_down


def per_layer_embed(h, p_i, g, w_up, w_gate):
    gate = jax.nn.sigmoid(rms_norm(h, g) @ w_gate)
    return (p_i @ w_up) * gate


def setup_inputs(seed: int = 0) -> dict:
    key = jax.random.key(seed)
    ks = jax.random.split(key, 20)
    f32 = jnp.float32

    def normal(k, shape, fan_in):
        return jax.random.normal(k, shape, f32) * (fan_in ** -0.5)

    def gain(k, shape):
        return 1.0 + 0.02 * jax.random.normal(k, shape, f32)

    ret_cols = 2 * D_MODEL + 2 * RET_VDIM
    return {
        "x": jax.random.normal(ks[0], (BATCH, SEQ, D_MODEL), f32),
        "p": jax.random.normal(ks[1], (DEPTH, BATCH, SEQ, PLE_DIM), f32),
        "ret_norm_g": gain(ks[2], (N_A_LAYERS, D_MODEL)),
        "ret_w_in": normal(ks[3], (N_A_LAYERS, D_MODEL, ret_cols), D_MODEL),
        "ret_w_out": normal(ks[4], (N_A_LAYERS, RET_VDIM, D_MODEL), RET_VDIM),
        "kv_norm_g": gain(ks[5], (D_MODEL,)),
        "w_kv": normal(ks[6], (D_MODEL, 2 * D_MODEL), D_MODEL),
        "att_norm_g": gain(ks[7], (N_B_LAYERS, D_MODEL)),
        "att_w_q": normal(ks[8], (N_B_LAYERS, D_MODEL, D_MODEL), D_MODEL),
        "att_w_o": normal(ks[9], (N_B_LAYERS, D_MODEL, D_MODEL), D_MODEL),
        "rel_bias": 0.5 * jax.random.normal(ks[10], (REL_BUCKETS, ATT_HEADS), f32),
        "mlp_norm_g": gain(ks[11], (DEPTH, D_MODEL)),
        "mlp_w_up": normal(ks[12], (DEPTH, D_MODEL, D_FF), D_MODEL),
        "mlp_w_down": normal(ks[13], (DEPTH, D_FF, D_MODEL), D_FF),
        "ple_norm_g": gain(ks[14], (DEPTH, D_MODEL)),
        "ple_w_up": normal(ks[15], (DEPTH, PLE_DIM, D_MODEL), PLE_DIM),
        "ple_w_gate": normal(ks[16], (DEPTH, D_MODEL, D_MODEL), D_MODEL),
        "final_norm_g": gain(ks[17], (D_MODEL,)),
    }


def reference(x, p, ret_norm_g, ret_w_in, ret_w_out, kv_norm_g, w_kv, att_norm_g, att_w_q,
              att_w_o, rel_bias, mlp_norm_g, mlp_w_up, mlp_w_down, ple_norm_g, ple_w_up,
              ple_w_gate, final_norm_g):
    h = x
    k_blocks = v_blocks = k_means = None
    for i in range(DEPTH):
        if i < N_A_LAYERS:
            h = h + retention(rms_norm(h, ret_norm_g[i]), ret_w_in[i], ret_w_out[i])
        else:
            j = i - N_A_LAYERS
            h = h + moba_attention(rms_norm(h, att_norm_g[j]), att_w_q[j], att_w_o[j],
                                   k_blocks, v_blocks, k_means, rel_bias)
        h = h + sq_relu_mlp(rms_norm(h, mlp_norm_g[i]), mlp_w_up[i], mlp_w_down[i])
        h = h + per_layer_embed(h, p[i], ple_norm_g[i], ple_w_up[i], ple_w_gate[i])
        if i == N_A_LAYERS - 1:
            k_blocks, v_blocks, k_means = shared_kv(h, kv_norm_g, w_kv)
    return rms_norm(h, final_norm_g)
```

```python
import math
import numpy as np
from contextlib import ExitStack
import concourse.bass as bass
import concourse.mybir as mybir
from concourse.bass_utils import run_bass_kernel_spmd

F32 = mybir.dt.float32
BF16 = mybir.dt.bfloat16
AF = mybir.ActivationFunctionType
ALU = mybir.AluOpType

ENGS = ("sync", "act", "dve", "pool", "pe")
T = 2048
D = 2048
NFB = 6


class Dummy:
    def __getitem__(self, k):
        return self

    def __getattr__(self, k):
        return self

    def __call__(self, *a, **k):
        return self


class Prog:
    def __init__(self, nc, dry=False, wplan=None, nfb=6, nbb=2, fences=None):
        self.nc = nc
        self.dry = dry
        self.es = ExitStack()
        self.streams = {e: [] for e in ENGS}
        self.semh = {}
        self.cnt = {}
        self.waited = {e: {} for e in ENGS}
        for e in ENGS:
            self.sem("m_" + e)
        self.wplan = wplan if wplan is not None else []
        self.wi = 0
        self.wissued = 0
        self.wrel = {}
        self.wtok = {}
        self.WS = 3
        self.wslots = [self.sb(f"wslot{i}", [128, 16, 512], BF16) for i in range(self.WS)]
        for i in range(self.WS):
            self.sem(f"wl{i}")
        self.nfb = nfb
        self.nbb = nbb
        self.nrot = nfb
        self.fences = fences if fences is not None else []
        self.opened = set()
        self.psf = [self.ps(f"psf{i}", [128, 512], F32) for i in range(nfb)]
        self.psb = [self.ps(f"psb{i}", [128, 1024], BF16) for i in range(nbb)]
        self.bfree = [None] * nfb
        self.bbfree = [None] * nbb
        self.bi = 0
        self.bbi = 0

    def sem(self, key):
        if key not in self.semh:
            self.semh[key] = None if self.dry else self.es.enter_context(self.nc.semaphore(key))
            self.cnt[key] = 0
        return key

    def sb(self, name, shape, dt):
        if self.dry:
            return Dummy()
        return self.es.enter_context(self.nc.sbuf_tensor("s_" + name, shape, dt))

    def ps(self, name, shape, dt=F32):
        if self.dry:
            return Dummy()
        return self.es.enter_context(self.nc.psum_tensor("p_" + name, shape, dt))

    def bank(self):
        b = self.bi % self.nrot
        self.bi += 1
        return b, self.psf[b], self.bfree[b]

    def bbank(self):
        b = self.bbi % self.nbb
        self.bbi += 1
        return b, self.psb[b], self.bbfree[b]

    def _waits(self, eng, waits):
        out = []
        w = self.waited[eng]
        for tok in waits:
            if tok is None:
                continue
            key, val = tok
            if w.get(key, 0) >= val:
                continue
            w[key] = val
            out.append((key, val))
        return out

    def op(self, eng, fn, waits=(), inc=False):
        if self.dry:
            return ("m_" + eng, 0)
        ws = self._waits(eng, waits)
        tok = None
        incd = None
        if inc:
            key = "m_" + eng
            self.cnt[key] += 1
            tok = (key, self.cnt[key])
            incd = (key, 1)
        self.streams[eng].append([ws, fn, incd])
        return tok

    def dma(self, eng, out, in_, semkey, waits=()):
        if self.dry:
            return (semkey, 0)
        self.sem(semkey)
        ws = self._waits(eng, waits)
        self.cnt[semkey] += 16
        tok = (semkey, self.cnt[semkey])
        self.streams[eng].append([ws, lambda e: e.dma_start(out=out, in_=in_), (semkey, 16)])
        return tok

    def wait_only(self, eng, waits):
        if self.dry:
            return
        ws = self._waits(eng, waits)
        if ws:
            self.streams[eng].append([ws, None, None])

    def barrier(self):
        if self.dry:
            return
        toks = []
        for e in ("act", "dve", "pe"):
            st = self.streams[e]
            for ent in reversed(st):
                if ent[1] is not None:
                    if ent[2] is None:
                        self.cnt["m_" + e] += 1
                        ent[2] = ("m_" + e, 1)
                    break
            toks.append(("m_" + e, self.cnt["m_" + e]))
        for key, c in self.cnt.items():
            if not key.startswith("m_") and c > 0:
                toks.append((key, c))
        for e in ENGS:
            self.wait_only(e, toks)

    def wnext(self, src, kc, ncols):
        if self.dry:
            self.wplan.append((src, kc, ncols))
            return Dummy(), None, len(self.wplan) - 1
        i = self.wi
        self.wi += 1
        assert self.wplan[i][1] == kc and self.wplan[i][2] == ncols
        self.wprefetch(i + self.WS - 1)
        return self.wslots[i % self.WS], self.wtok[i], i

    def wfence(self):
        if self.dry:
            self.fences.append(len(self.wplan))

    def wopen(self):
        if not self.dry:
            self.opened.add(self.wi)

    def wprefetch(self, j):
        if self.dry:
            return
        while self.wissued <= min(j, len(self.wplan) - 1):
            s = self.wissued
            if s in self.fences and s not in self.opened:
                break
            slot = s % self.WS
            src, kc, ncols = self.wplan[s]
            waits = []
            if s - self.WS >= 0:
                if (s - self.WS) not in self.wrel:
                    assert s > j - (self.WS - 1), "needed slab cannot be issued"
                    break
                waits.append(self.wrel[s - self.WS])
            self.wtok[s] = self.dma("pool", self.wslots[slot][:, 0:kc, 0:ncols], src, f"wl{slot}", waits)
            self.wissued += 1

    def wrelease(self, i, tok):
        if self.dry:
            return
        self.wrel[i] = tok

    def mm(self, out, lhsT, rhs, start, stop, waits=(), inc=False):
        return self.op("pe", lambda e: e.matmul(out, lhsT=lhsT, rhs=rhs, start=start, stop=stop), waits, inc)

    def tr(self, out, in_, ident, waits=(), inc=False):
        return self.op("pe", lambda e: e.transpose(out, in_, ident), waits, inc)

    def act(self, out, in_, func, waits=(), inc=True, **kw):
        return self.op("act", lambda e: e.activation(out=out, in_=in_, func=func, **kw), waits, inc)

    def tt(self, out, in0, in1, op, waits=(), inc=True, eng="dve"):
        return self.op(eng, lambda e: e.tensor_tensor(out=out, in0=in0, in1=in1, op=op), waits, inc)

    def ts(self, out, in0, s1, s2, op0, op1=None, waits=(), inc=True, eng="dve"):
        if op1 is None:
            return self.op(eng, lambda e: e.tensor_scalar(out=out, in0=in0, scalar1=s1, scalar2=None, op0=op0), waits, inc)
        return self.op(eng, lambda e: e.tensor_scalar(out=out, in0=in0, scalar1=s1, scalar2=s2, op0=op0, op1=op1), waits, inc)

    def stt(self, out, in0, scalar, in1, op0, op1, waits=(), inc=True, eng="dve"):
        return self.op(eng, lambda e: e.scalar_tensor_tensor(out=out, in0=in0, scalar=scalar, in1=in1, op0=op0, op1=op1), waits, inc)

    def emit(self):
        nc = self.nc
        with nc.Block() as block:
            def run(stream):
                def f(e):
                    for ws, fn, inc in stream:
                        for key, val in ws:
                            e.wait_ge(self.semh[key], val)
                        if fn is None:
                            continue
                        ins = fn(e)
                        if inc is not None:
                            ins.then_inc(self.semh[inc[0]], inc[1])
                return f
            block.sync(run(self.streams["sync"]))
            block.scalar(run(self.streams["act"]))
            block.vector(run(self.streams["dve"]))
            block.gpsimd(run(self.streams["pool"]))
            block.tensor(run(self.streams["pe"]))
        self.es.close()


def wslab(W, r0, kc, c0, ncols=512):
    return W[r0:r0 + kc * 128, c0:c0 + ncols].rearrange("(kc p) n -> p kc n", p=128)


class Ring:
    def __init__(self, p, name, n, shape, dt, tiles=None):
        self.p = p
        self.n = n
        self.tiles = tiles if tiles is not None else [p.sb(f"{name}{i}", shape, dt) for i in range(n)]
        self.sems = [p.sem(f"{name}_s{i}") for i in range(n)]
        self.free = [None] * n
        self.i = 0

    def next(self):
        s = self.i % self.n
        self.i += 1
        return s, self.tiles[s], self.sems[s], self.free[s]


class Ctx:
    pass


def setup_common(p, nc, c, gdram, n_g):
    c.A = p.sb("A", [128, 16, T], BF16)
    c.arena = p.sb("B", [128, 16 * T], BF16)
    ar = c.arena
    c.B = ar[:, :].rearrange("p (k t) -> p k t", k=16)
    c.g = p.sb("gains", [128, n_g, 16], F32)
    c.ones = p.sb("ones", [128, 128], BF16)
    c.ident = p.sb("ident", [128, 128], BF16)
    c.hring = Ring(p, "hr", 2, None, None, tiles=[ar[:, i * 4096:(i + 1) * 4096].bitcast(F32) for i in range(2)])
    c.sqring = Ring(p, "sq", 2, None, None, tiles=[ar[:, 8192 + i * 2048:8192 + (i + 1) * 2048] for i in range(2)])
    c.rstd = ar[:, 12288:16384].bitcast(F32)
    c.rring = Ring(p, "rr", 3, [128, 512], F32)
    c.tmpring = Ring(p, "tmp", 4, [128, 512], F32)
    c.oring = Ring(p, "obf", 4, [128, 512], BF16)
    c.htok = {}
    t1 = p.dma("sync", c.g[:, :, :], gdram, p.sem("cst"))
    t2 = p.dma("pool", c.ident[:, :], c.ident_dram, p.sem("cst"))
    c.eps = p.sb("eps", [128, 2], F32)
    p.op("dve", lambda e: e.memset(c.eps[:, 0:1], 1e-6))
    p.op("dve", lambda e: e.memset(c.eps[:, 1:2], 1e-5))
    t3 = p.op("dve", lambda e: e.memset(c.ones[:, :], 1.0), inc=True)
    c.cst = [t1, t2, t3]


def norm_phase(p, c, hT, gi_dsts, out_dram=None):
    banks = [p.bank() for _ in range(4)]
    gi, dst = gi_dsts[0]
    last_mm = None
    thg = None
    for ch in range(16):
        s, ht, hs, hfree = c.hring.next()
        tl = p.dma("sync", ht[:, :], hT[ch * 128:(ch + 1) * 128, :], hs, waits=[hfree])
        s2, sq, _, sqfree = c.sqring.next()
        tsq = p.act(sq[:, :], ht[:, :], AF.Square, waits=[tl, sqfree])
        if out_dram is None:
            thg = p.ts(dst[:, ch, :], ht[:, :], c.g[:, gi, ch:ch + 1], None, ALU.mult, waits=[tl, tsq] + c.cst)
            c.hring.free[s] = thg
        else:
            c.hring.free[s] = tsq
        for tg in range(4):
            b, pt, bf = banks[tg]
            last_mm = p.mm(pt[:, :], c.ones[:, :], sq[:, tg * 512:(tg + 1) * 512], ch == 0, ch == 15,
                           waits=[tsq, bf] + c.cst, inc=(tg == 3))
        c.sqring.free[s2] = last_mm
    tr = None
    for tg in range(4):
        b, pt, bf = banks[tg]
        rs_ = c.rstd[:, tg * 512:(tg + 1) * 512]
        tsq_ = p.act(rs_, pt[:, :], AF.Sqrt, waits=[last_mm] + c.cst, scale=1.0 / D, bias=c.eps[:, 0:1])
        p.bfree[b] = tsq_
        tr = p.op("dve", lambda e, rs_=rs_: e.reciprocal(out=rs_, in_=rs_), waits=[tsq_], inc=True)
    last = None
    if out_dram is None:
        for ch in range(16):
            last = p.tt(dst[:, ch, :], dst[:, ch, :], c.rstd[:, :], ALU.mult, waits=[tr, thg])
        return last
    for ch in range(16):
        s, ht, hs, hfree = c.hring.next()
        tl = p.dma("sync", ht[:, :], hT[ch * 128:(ch + 1) * 128, :], hs, waits=[hfree])
        last = p.stt(ht[:, :], ht[:, :], c.g[:, gi, ch:ch + 1], c.rstd[:, :], ALU.mult, ALU.mult,
                     waits=[tl, tr] + c.cst)
        tst = p.dma("sync", out_dram[ch * 128:(ch + 1) * 128, :], ht[:, :], hs, waits=[last])
        c.hring.free[s] = tst
        c.out_toks.append(tst)
    return last


def proj_fm(p, c, act, W, r0, kc, c0, ncols_total, handler, pre=None, order="tg_oc"):
    nslab = ncols_total // 512
    for s in range(nslab):
        wt, wtok, wi = p.wnext(wslab(W, r0, kc, c0 + s * 512), kc, 512)
        last = None
        for tg in range(4):
            for oc in range(4):
                ocg = s * 4 + oc
                if pre is not None:
                    pre(ocg, tg)
                b, pt, bf = p.bank()
                for k in range(kc):
                    last = p.mm(pt[:, :], wt[:, k, oc * 128:(oc + 1) * 128], act[:, k, tg * 512:(tg + 1) * 512],
                                k == 0, k == kc - 1, waits=[wtok, bf, c.act_ready] if k == 0 else (), inc=(k == kc - 1))
                handler(ocg, tg, b, pt, last)
        p.wrelease(wi, last)


def proj_tm(p, c, act, W, r0, kc, c0, ncols_total, handler):
    nslab = ncols_total // 512
    for s in range(nslab):
        wt, wtok, wi = p.wnext(wslab(W, r0, kc, c0 + s * 512), kc, 512)
        last = None
        for tt in range(16):
            b, pt, bf = p.bank()
            for k in range(kc):
                last = p.mm(pt[:, :], act[:, k, tt * 128:(tt + 1) * 128], wt[:, k, :],
                            k == 0, k == kc - 1, waits=[wtok, bf, c.act_ready] if k == 0 else (), inc=(k == kc - 1))
            handler(s, tt, b, pt, last)
        p.wrelease(wi, last)


class Residual:
    def __init__(self, p, c, src, dst):
        self.p, self.c, self.src, self.dst = p, c, src, dst
        self.pending = {}

    def pre(self, ocg, tg):
        p, c = self.p, self.c
        s, rt, rs, rfree = c.rring.next()
        tl = p.dma("sync", rt[:, :], self.src[ocg * 128:(ocg + 1) * 128, tg * 512:(tg + 1) * 512], rs,
                   waits=[rfree, c.htok.get((ocg, tg))])
        self.pending[(ocg, tg)] = (s, rt, rs, tl)

    def post(self, ocg, tg, b, pt, petok):
        p, c = self.p, self.c
        s, rt, rs, tl = self.pending.pop((ocg, tg))
        ta = p.tt(rt[:, :], pt[:, :], rt[:, :], ALU.add, waits=[petok, tl])
        p.bfree[b] = ta
        tst = p.dma("sync", self.dst[ocg * 128:(ocg + 1) * 128, tg * 512:(tg + 1) * 512], rt[:, :], rs, waits=[ta])
        c.rring.free[s] = tst
        c.htok[(ocg, tg)] = tst


def mlp_phase(p, c, hT, W1, W2, tmpring):
    for q in range(4):
        def up_handler(ocg, tg, b, pt, petok):
            s, tt_, _, tfree = tmpring.next()
            t1 = p.act(tt_[:, :], pt[:, :], AF.Relu, waits=[petok, tfree])
            p.bfree[b] = t1
            t2 = p.tt(c.B[:, ocg, tg * 512:(tg + 1) * 512], tt_[:, :], tt_[:, :], ALU.mult, waits=[t1, c.b_free])
            tmpring.free[s] = t2
            c.b_last = t2
        c.act_ready = c.a_ready
        proj_fm(p, c, c.A, W1, 0, 16, q * 2048, 2048, up_handler)
        c.act_ready = c.b_last
        res = Residual(p, c, hT, hT)
        proj_fm(p, c, c.B, W2, q * 2048, 16, 0, 2048, res.post, pre=res.pre)
        c.b_free = _pe_token(p)


def _pe_token(p):
    if p.dry:
        return None
    st = p.streams["pe"]
    for ent in reversed(st):
        if ent[1] is not None:
            if ent[2] is None:
                p.cnt["m_pe"] += 1
                ent[2] = ("m_pe", 1)
            break
    return ("m_pe", p.cnt["m_pe"])


def ple_phase(p, c, hT, Wg, Wu, pT_sb, tmpring):
    for s in range(4):
        wg, wgtok, wgi = p.wnext(wslab(Wg, 0, 16, s * 512), 16, 512)
        wu, wutok, wui = p.wnext(wslab(Wu, 0, 2, s * 512), 2, 512)
        last = None
        for tg in range(4):
            for oc in range(4):
                ocg = s * 4 + oc
                sr, rt, rs, rfree = c.rring.next()
                tl = p.dma("sync", rt[:, :], hT[ocg * 128:(ocg + 1) * 128, tg * 512:(tg + 1) * 512], rs,
                           waits=[rfree, c.htok.get((ocg, tg))])
                b, pt, bf = p.bank()
                for k in range(16):
                    last = p.mm(pt[:, :], wg[:, k, oc * 128:(oc + 1) * 128], c.A[:, k, tg * 512:(tg + 1) * 512],
                                k == 0, k == 15, waits=[wgtok, bf, c.a_ready] if k == 0 else (), inc=(k == 15))
                st, gt, _, tfree = tmpring.next()
                tg_ = p.act(gt[:, :], pt[:, :], AF.Sigmoid, waits=[last, tfree])
                p.bfree[b] = tg_
                b2, pt2, bf2 = p.bank()
                for k in range(2):
                    last = p.mm(pt2[:, :], wu[:, k, oc * 128:(oc + 1) * 128], pT_sb[:, k, tg * 512:(tg + 1) * 512],
                                k == 0, k == 1, waits=[wutok, bf2, c.p_ready] if k == 0 else (), inc=(k == 1))
                tm = p.tt(gt[:, :], pt2[:, :], gt[:, :], ALU.mult, waits=[last, tg_])
                p.bfree[b2] = tm
                ta = p.tt(rt[:, :], rt[:, :], gt[:, :], ALU.add, waits=[tl, tm])
                tmpring.free[st] = ta
                tst = p.dma("sync", hT[ocg * 128:(ocg + 1) * 128, tg * 512:(tg + 1) * 512], rt[:, :], rs, waits=[ta])
                c.rring.free[sr] = tst
                c.htok[(ocg, tg)] = tst
        p.wrelease(wgi, last)
        p.wrelease(wui, last)


GI_RET, GI_MLP0, GI_PLE0, GI_KV, GI_ATT, GI_MLP1, GI_PLE1, GI_FIN = range(8)


def l0_setup(p, c, dr):
    c.consts = p.sb("consts", [128, 16], F32)
    c.decT = p.sb("decT", [128, 8, 128], F32)
    tcst = [p.dma("sync", c.consts[:, :], dr["zx"], p.sem("cst")),
            p.dma("sync", c.decT[:, :, :], dr["decT"], p.sem("cst"))]
    c.cst = c.cst + tcst
    c.kzr = Ring(p, "kz", 2, [128, 256], BF16)
    c.sTr = Ring(p, "sT", 2, [128, 128], BF16)
    c.stat = Ring(p, "stat", 2, [128, 16], F32)
    c.km = p.sb("km", [128, 16, 8], F32)


def l0_half(p, c, dr, half):
    xT = dr["xT2"][half]
    hT = dr["hT2"][half]
    c.htok = {}
    tmpring = c.tmpring
    oring = c.oring
    consts, decT = c.consts, c.decT
    cs_cos, cs_sin = c.hring.tiles[0], c.hring.tiles[1]
    pT_sb = c.arena[:, 24576:28672].rearrange("p (k t) -> p k t", k=2)
    c.b_free = None
    qT_s, kT_s, v_s, sg_s = dr["qT_s"], dr["kT_s"], dr["v_s"], dr["sg_s"]
    W_in = dr["w_in"]

    def rot_handler(dst):
        state = {}

        def h(ocg, tg, b, pt, petok):
            if ocg % 2 == 0:
                state[tg] = (b, pt, petok)
                return
            b1, x1, tk1 = state.pop(tg)
            b2, x2, tk2 = b, pt, petok
            cos = cs_cos[:, tg * 512:(tg + 1) * 512]
            sin = cs_sin[:, tg * 512:(tg + 1) * 512]
            s1, ta, _, f1 = tmpring.next()
            s2, tb, _, f2 = tmpring.next()
            so1, o1, os1, of1 = oring.next()
            so2, o2, os2, of2 = oring.next()
            w = [tk1, tk2, c.cs_ready]
            p.tt(ta[:, :], x1[:, :], cos, ALU.mult, waits=w + [f1])
            p.tt(tb[:, :], x2[:, :], sin, ALU.mult, waits=[f2])
            t_o1 = p.tt(o1[:, :], ta[:, :], tb[:, :], ALU.subtract, waits=[of1])
            p.tt(ta[:, :], x1[:, :], sin, ALU.mult)
            t4 = p.tt(tb[:, :], x2[:, :], cos, ALU.mult)
            p.bfree[b1] = t4
            p.bfree[b2] = t4
            t_o2 = p.tt(o2[:, :], ta[:, :], tb[:, :], ALU.add, waits=[of2])
            tmpring.free[s1] = t_o2
            tmpring.free[s2] = t_o2
            r1 = (ocg - 1) * 128
            oring.free[so1] = p.dma("sync", dst[r1:r1 + 128, tg * 512:(tg + 1) * 512], o1[:, :], os1, waits=[t_o1])
            oring.free[so2] = p.dma("sync", dst[r1 + 128:r1 + 256, tg * 512:(tg + 1) * 512], o2[:, :], os2, waits=[t_o2])
        return h

    def tm_handler(dst, func):
        def h(s, tt, b, pt, petok):
            so, o, osm, of = oring.next()
            t1 = p.act(o[:, :], pt[:, :], func, waits=[petok, of])
            p.bfree[b] = t1
            oring.free[so] = p.dma("sync", dst[tt * 128:(tt + 1) * 128, s * 512:(s + 1) * 512], o[:, :], osm, waits=[t1])
        return h

    c.a_ready = norm_phase(p, c, xT, [(GI_RET, c.A)])
    c.act_ready = c.a_ready
    p.barrier()
    p.dma("sync", cs_cos[:, :], dr["cs2"][half, 0], p.sem("csl"))
    c.cs_ready = p.dma("sync", cs_sin[:, :], dr["cs2"][half, 1], p.sem("csl"))
    proj_fm(p, c, c.A, W_in, 0, 16, 0, 2048, rot_handler(qT_s))
    proj_fm(p, c, c.A, W_in, 0, 16, 2048, 2048, rot_handler(kT_s))
    proj_tm(p, c, c.A, W_in, 0, 16, 4096, 4096, tm_handler(v_s, AF.Copy))
    proj_tm(p, c, c.A, W_in, 0, 16, 8192, 4096, tm_handler(sg_s, AF.Silu))
    p.barrier()
    ar = c.arena
    S32f = ar[:, 0:8192].bitcast(F32)
    S32 = S32f.rearrange("p (j d e) -> p j d e", j=4, d=2)
    Sbf2 = ar[:, 8192:12288]
    Sbf = Sbf2.rearrange("p (j d e) -> p j d e", j=4, d=2)
    o = 12288
    kring = Ring(p, "kch", 3, None, None, tiles=[ar[:, o + i * 1024:o + (i + 1) * 1024].rearrange("p (f t) -> p f t", f=8) for i in range(3)])
    o += 3072
    qring = Ring(p, "qch", 3, None, None, tiles=[ar[:, o + i * 1024:o + (i + 1) * 1024].rearrange("p (f t) -> p f t", f=8) for i in range(3)])
    o += 3072
    vring = Ring(p, "vch", 3, None, None, tiles=[ar[:, o + i * 2048:o + (i + 1) * 2048] for i in range(3)])
    o += 6144
    gring = Ring(p, "gch", 3, None, None, tiles=[ar[:, o + i * 2048:o + (i + 1) * 2048] for i in range(3)])
    kzr, sTr, stat = c.kzr, c.sTr, c.stat
    yr = tmpring
    ygr = oring
    gam = [1.0 - 2.0 ** (-5.0 - h) for h in range(8)]
    pbk = p.psb[0]
    pby = p.psf[6][:, :].bitcast(BF16)
    bfk = [None]
    bfy = [None]
    p.nrot = 6
    for hp in range(2):
        h0 = hp * 4
        if half == 0:
            tz = p.op("dve", lambda e: e.memset(S32f[:, :], 0.0), inc=True)
            tzb = p.op("dve", lambda e: e.memset(Sbf2[:, :], 0.0), inc=True)
        else:
            tld = p.dma("sync", S32f[:, :], dr["Sst"][hp], p.sem("sst"))
            tz = tld
            tzb = p.act(Sbf2[:, :], S32f[:, :], AF.Copy, waits=[tld])
        sb_tok = [tzb] * 4
        s32_tok = [tz] * 4
        inter_tok = [None] * 4
        a_last = None
        for n in range(16):
            no = n
            ks, kt, ksem, kfree = kring.next()
            vs, vt, vsem, vfree = vring.next()
            qs, qt, qsem, qfree = qring.next()
            gs, gt, gsem, gfree = gring.next()
            tk = p.dma("sync", kt[:, :, :], kT_s[h0 * 256:h0 * 256 + 1024, n * 128:(n + 1) * 128].rearrange("(f p) t -> p f t", p=128),
                       ksem, waits=[kfree])
            tv = p.dma("sync", vt[:, :], v_s[n * 128:(n + 1) * 128, h0 * 512:h0 * 512 + 2048], vsem, waits=[vfree])
            tq = p.dma("sync", qt[:, :, :], qT_s[h0 * 256:h0 * 256 + 1024, n * 128:(n + 1) * 128].rearrange("(f p) t -> p f t", p=128),
                       qsem, waits=[qfree])
            tgl = p.dma("sync", gt[:, :], sg_s[n * 128:(n + 1) * 128, h0 * 512:h0 * 512 + 2048], gsem, waits=[gfree])
            lastg = None
            pendB = None
            for j in range(4):
                h = h0 + j
                pb = pbk
                p.tr(pb[:, 0:128], kt[:, 2 * j, :], c.ident[:, :], waits=[tk, bfk[0]] + c.cst)
                ttr = p.tr(pb[:, 128:256], kt[:, 2 * j + 1, :], c.ident[:, :], inc=True)
                zs, kz, _, kzfree = kzr.next()
                tkz = p.act(kz[:, :], pb[:, 0:256], AF.Copy, waits=[ttr, kzfree], scale=consts[:, h:h + 1])
                bfk[0] = tkz
                b1, ps_s, bf1 = p.bank()
                p.mm(ps_s[:, 0:128], kt[:, 2 * j, :], qt[:, 2 * j, :], True, False, waits=[tq, bf1])
                tsc = p.mm(ps_s[:, 0:128], kt[:, 2 * j + 1, :], qt[:, 2 * j + 1, :], False, True, inc=True)
                ss, sT, _, sTfree = sTr.next()
                tsT = p.tt(sT[:, :], ps_s[:, 0:128], decT[:, h, :], ALU.mult, waits=[tsc, sTfree])
                p.bfree[b1] = tsT
                b3, ps_x, bf3 = p.bank()
                p.mm(ps_x[:, :], qt[:, 2 * j, :], Sbf[:, j, 0, :], True, False, waits=[sb_tok[j], bf3])
                tix = p.mm(ps_x[:, :], qt[:, 2 * j + 1, :], Sbf[:, j, 1, :], False, True, inc=True)
                inter_tok[j] = tix
                b2, ps_i, bf2 = p.bank()
                tin = p.mm(ps_i[:, :], sT[:, :], vt[:, j * 512:(j + 1) * 512], True, True, waits=[tsT, tv, bf2], inc=True)
                sTr.free[ss] = tin
                ys, y, _, yfree = yr.next()
                ygs, yg, _, ygfree = ygr.next()
                sts, stt_, _, stfree = stat.next()
                t_x = p.act(y[:, :], ps_x[:, :], AF.Copy, waits=[tix, yfree], scale=consts[:, 8 + h:9 + h])
                p.bfree[b3] = t_x
                t_y = p.tt(y[:, :], ps_i[:, :], y[:, :], ALU.add, waits=[tin, t_x])
                p.bfree[b2] = t_y
                t_bs = p.op("dve", lambda e, stt_=stt_, y=y: e.bn_stats(out=stt_[:, 0:6], in_=y[:, :]), waits=[stfree], inc=True)
                t_ag = p.op("dve", lambda e, stt_=stt_: e.bn_aggr(out=stt_[:, 8:10], in_=stt_[:, 0:6]), waits=[t_bs], inc=True)
                t_sq = p.act(stt_[:, 9:10], stt_[:, 9:10], AF.Sqrt, waits=[t_ag], bias=c.eps[:, 1:2])
                tst_last = None
                st_banks = []
                for dc in range(2):
                    b4, ps_st, bf4 = p.bank()
                    tmm = p.mm(ps_st[:, :], kz[:, dc * 128:(dc + 1) * 128], vt[:, j * 512:(j + 1) * 512], True, True,
                               waits=[tkz, tv, bf4], inc=True)
                    st_banks.append((b4, ps_st, tmm))
                    tst_last = tmm
                kzr.free[zs] = tst_last
                t_rc = p.op("dve", lambda e, stt_=stt_: e.reciprocal(out=stt_[:, 9:10], in_=stt_[:, 9:10]), waits=[t_sq], inc=True)
                p.wait_only("dve", [t_rc])
                p.ts(y[:, :], y[:, :], stt_[:, 8:9], stt_[:, 9:10], ALU.subtract, ALU.mult)
                t_yg = p.tt(yg[:, :], y[:, :], gt[:, j * 512:(j + 1) * 512], ALU.mult, waits=[tgl, ygfree])
                yr.free[ys] = t_yg
                stat.free[sts] = t_yg
                lastg = t_yg
                tup = None
                for dc in range(2):
                    b4, ps_st, tmm = st_banks[dc]
                    tup = p.stt(S32[:, j, dc, :], S32[:, j, dc, :], float(gam[h] ** 128), ps_st[:, :], ALU.mult, ALU.add,
                                waits=[tmm, s32_tok[j]])
                    p.bfree[b4] = tup
                s32_tok[j] = tup
                sb_tok[j] = p.act(Sbf[:, j, :, :], S32[:, j, :, :], AF.Copy, waits=[tup, inter_tok[j]])

                def stageB(j=j, yg=yg, ygs=ygs, t_yg=t_yg):
                    ttr2 = None
                    for i in range(4):
                        ttr2 = p.tr(pby[:, i * 128:(i + 1) * 128], yg[:, i * 128:(i + 1) * 128], c.ident[:, :],
                                    waits=[t_yg, bfy[0]] if i == 0 else (), inc=(i == 3))
                    ygr.free[ygs] = ttr2
                    al = p.act(c.A[:, j * 4:(j + 1) * 4, no * 128:(no + 1) * 128],
                               pby[:, 0:512].rearrange("p (i t) -> p i t", i=4), AF.Copy, waits=[ttr2])
                    bfy[0] = al
                    return al
                if pendB is not None:
                    a_last = pendB()
                pendB = stageB
            a_last = pendB()
            kring.free[ks] = _pe_token(p)
            vring.free[vs] = kring.free[ks]
            qring.free[qs] = kring.free[ks]
            gring.free[gs] = lastg
        if half == 0:
            p.dma("sync", dr["Sst"][hp], S32f[:, :], p.sem("sst"), waits=s32_tok)
        c.act_ready = a_last
        res = Residual(p, c, xT if hp == 0 else hT, hT)
        proj_fm(p, c, c.A, dr["w_out"], hp * 2048, 16, 0, 2048, res.post, pre=res.pre)
        p.barrier()
    p.nrot = p.nfb
    c.a_ready = norm_phase(p, c, hT, [(GI_MLP0, c.A)])
    p.barrier()
    mlp_phase(p, c, hT, dr["w_up"], dr["w_dn"], tmpring)
    p.barrier()
    c.p_ready = p.dma("pool", pT_sb[:, :, :], dr["pT2"][half].rearrange("(kc p) t -> p kc t", p=128), p.sem("pld"))
    c.a_ready = norm_phase(p, c, hT, [(GI_PLE0, c.A)])
    ple_phase(p, c, hT, dr["pw_gate"], dr["pw_up"], pT_sb, tmpring)
    p.barrier()
    c.a_ready = norm_phase(p, c, hT, [(GI_KV, c.A)])
    c.act_ready = c.a_ready
    km = c.km

    def k_handler(ocg, tg, b, pt, petok):
        so, o, osm, of = oring.next()
        t1 = p.act(o[:, :], pt[:, :], AF.Copy, waits=[petok, of])
        t2 = p.op("dve", lambda e: e.tensor_reduce(out=km[:, ocg, tg * 2:tg * 2 + 2], in_=pt[:, :].rearrange("p (b t) -> p b t", b=2),
                                                    axis=mybir.AxisListType.X, op=ALU.add), waits=[petok, t1], inc=True)
        p.bfree[b] = t2
        c.km_last = t2
        oring.free[so] = p.dma("sync", dr["kTa"][ocg * 128:(ocg + 1) * 128, half * T + tg * 512:half * T + (tg + 1) * 512], o[:, :], osm, waits=[t1])

    def v_handler(s, tt, b, pt, petok):
        so, o, osm, of = oring.next()
        t1 = p.act(o[:, :], pt[:, :], AF.Copy, waits=[petok, of])
        p.bfree[b] = t1
        oring.free[so] = p.dma("sync", dr["va"][half * T + tt * 128:half * T + (tt + 1) * 128, s * 512:(s + 1) * 512], o[:, :], osm, waits=[t1])

    proj_fm(p, c, c.A, dr["w_kv"], 0, 16, 0, 2048, k_handler)
    proj_tm(p, c, c.A, dr["w_kv"], 0, 16, 2048, 2048, v_handler)
    tkm = p.ts(km[:, :, :], km[:, :, :], 1.0 / 256.0, None, ALU.mult, waits=[c.km_last])
    p.dma("sync", dr["kma"][:, :, half * 8:(half + 1) * 8], km[:, :, :], p.sem("kmst"), waits=[tkm])
    p.barrier()


def build_fused(p, nc, dr):
    c = Ctx()
    c.ident_dram = dr["ident"]
    c.out_toks = []
    setup_common(p, nc, c, dr["gains"], 8)
    l0_setup(p, c, dr)
    for half in range(2):
        l0_half(p, c, dr, half)
    build_l1(p, nc, dr, c=c, hT_in=dr["hT2"][1], kTa=dr["kTa"], va=dr["va"])


def declare_fused(nc):
    dr = {}

    def inp(name, shape, dt=F32):
        dr[name] = nc.dram_tensor(name, shape, dt, kind="ExternalInput").ap()

    def outp(name, shape, dt):
        dr[name] = nc.dram_tensor(name, shape, dt, kind="ExternalOutput").ap()

    def scr(name, shape, dt):
        dr[name] = nc.dram_tensor(name, shape, dt).ap()

    inp("xT2", [2, D, T]); inp("pT2", [2, 256, T]); inp("cs2", [2, 2, 128, T])
    inp("gains", [128, 8, 16]); inp("ident", [128, 128]); inp("zx", [128, 16]); inp("decT", [128, 8, 128])
    inp("w_in", [D, 12288]); inp("w_out", [4096, D]); inp("w_up", [D, 8192]); inp("w_dn", [8192, D])
    inp("pw_up", [256, D]); inp("pw_gate", [D, D]); inp("w_kv", [D, 4096])
    inp("pT1", [256, T])
    inp("toep", [16, 128, 14 * 128]); inp("pastb", [128, 16, 16]); inp("b31", [128, 16]); inp("esel", [128, 16, 128])
    inp("w_q", [D, D]); inp("w_o", [D, D]); inp("w_up1", [D, 8192]); inp("w_dn1", [8192, D])
    inp("pw_up1", [256, D]); inp("pw_gate1", [D, D])
    outp("outT", [D, T], F32)
    scr("hT2", [2, D, T], F32); scr("hs", [D, T], F32)
    scr("qT_s", [D, T], BF16); scr("kT_s", [D, T], BF16); scr("v_s", [T, 4096], BF16); scr("sg_s", [T, 4096], BF16)
    scr("Sst", [2, 128, 4096], F32)
    scr("kTa", [D, 2 * T], BF16); scr("va", [2 * T, D], BF16); scr("kma", [128, 16, 16], F32)
    return dr


def build_nc_fused():
    nc = bass.Bass("TRN2", target_bir_lowering=False)
    dr = declare_fused(nc)
    pd = Prog(nc, dry=True, nfb=7, nbb=1)
    build_fused(pd, nc, dr)
    p = Prog(nc, dry=False, wplan=pd.wplan, nfb=7, nbb=1, fences=pd.fences)
    build_fused(p, nc, dr)
    p.emit()
    return nc, p


def gain_layout(gs):
    return np.ascontiguousarray(np.stack([g.reshape(16, 128).T for g in gs], axis=1)).astype(np.float32)


def rot_tables(pos0):
    half = 128
    inv = (10000.0 ** (-np.arange(half, dtype=np.float32) / half)).astype(np.float32)
    pos = (pos0 + np.arange(T)).astype(np.float32)
    ang = (pos[None, :] * inv[:, None]).astype(np.float32)
    return np.stack([np.cos(ang), np.sin(ang)]).astype(np.float32)


def ret_consts():
    lg = np.log1p(-np.exp2(-5.0 - np.arange(8, dtype=np.float64)))
    idx = np.arange(128)
    zx = np.zeros((128, 16), np.float32)
    decT = np.zeros((128, 8, 128), np.float32)
    for h in range(8):
        zx[:, h] = np.exp((127 - idx) * lg[h])
        zx[:, 8 + h] = np.exp((idx + 1) * lg[h]) / 16.0
        diff = idx[None, :] - idx[:, None]
        decT[:, h, :] = np.where(diff >= 0, np.exp(np.maximum(diff, 0) * lg[h]), 0.0) / 16.0
    return zx, decT


NEG = -1.0e30


def attention_phase(p, c, dr, kTa, va):
    A, Q = c.A, c.B
    ws = p.wslots
    kbuf = [ws[0][:, 0:8, :].rearrange("p a b -> p (a b)"), ws[0][:, 8:16, :].rearrange("p a b -> p (a b)")]
    vbuf = [ws[1][:, 0:8, :].rearrange("p a (t d) -> p (a t) d", d=128), ws[1][:, 8:16, :].rearrange("p a (t d) -> p (a t) d", d=128)]
    w2 = ws[2][:, :, :].rearrange("p a b -> p (a b)")
    toep = w2[:, 0:3584].bitcast(F32)
    maskT = w2[:, 3584:5632]
    ptr = Ring(p, "pt", 4, None, None, tiles=[w2[:, 5632 + i * 512:5632 + (i + 1) * 512] for i in range(4)])
    ksem = [p.sem("kh0"), p.sem("kh1")]
    vsem = [p.sem("vh0"), p.sem("vh1")]
    tsem = p.sem("toep")
    kfree = [None, None]
    toep_free = None
    gm = p.sb("gm", [128, 16, 16], F32)
    top8 = p.sb("top8", [128, 16, 8], F32)
    thr = p.sb("thr", [128, 16], F32)
    mb = p.sb("mb", [128, 16, 16], BF16)
    OT = [p.psf[3], p.psf[4]]
    DEN = [p.psf[5], p.psf[6]]
    od_free = [None, None]
    accP = [c.rring.tiles[0], c.rring.tiles[1]]
    ones32 = c.rring.tiles[2][:, 0:128]
    p.op("dve", lambda e: e.memset(maskT[:, :], 0.0))
    t_ones32 = p.op("dve", lambda e: e.memset(ones32, 1.0), inc=True)
    acc_free = [None, None]
    acc_tok = [None, None]
    pt_free2 = [None] * 4
    p.nrot = 3
    mask_free = None
    gm_free = None
    mb_free = None
    for h in range(16):
        s = h % 2
        tk = p.dma("sync", kbuf[s][:, :], kTa[h * 128:(h + 1) * 128, :], ksem[s], waits=[kfree[s]])
        tv = p.dma("sync", vbuf[s][:, :, :], va[:, h * 128:(h + 1) * 128].rearrange("(t p) d -> p t d", p=128), vsem[s], waits=[kfree[s]])
        tt_ = p.dma("sync", toep[:, :], dr["toep"][h], tsem, waits=[toep_free])
        b, gps, bf = p.bank()
        tg = None
        for qt in range(16):
            tg = p.mm(gps[:, qt * 16:(qt + 1) * 16], Q[:, h, qt * 128:(qt + 1) * 128], c.kmb[:, h, :], True, True,
                      waits=[bf, c.q_ready, c.kmb_ready] if qt == 0 else (), inc=(qt == 15))
        t1 = p.tt(gm[:, :, :], gps[:, 0:256].rearrange("p (a b) -> p a b", a=16), c.pastb[:, :, :], ALU.add, waits=[tg, gm_free] + c.cst2)
        p.bfree[b] = t1
        tm = None
        for qt in range(16):
            tm = p.op("dve", lambda e, qt=qt: e.max(out=top8[:, qt, :], in_=gm[:, qt, :]), waits=[t1], inc=True)
        t2 = p.ts(thr[:, :], top8[:, :, 2], -1.0e29, None, ALU.max, waits=[tm])
        t3 = None
        for qt in range(16):
            t3 = p.ts(mb[:, qt, :], gm[:, qt, :], thr[:, qt:qt + 1], NEG, ALU.is_lt, ALU.mult, waits=[t2, mb_free])
        for qt in range(16):
            jj = 8 + qt // 2
            t3 = p.op("dve", lambda e, qt=qt, jj=jj: e.memset(mb[:, qt, jj:jj + 1], 0.0), waits=[t3], inc=True)
        gm_free = t3
        tcp = None
        for half in range(2):
            bb, pb, bbf = p.bbank()
            ttr = None
            for q8 in range(8):
                qt = half * 8 + q8
                ttr = p.tr(pb[0:16, q8 * 128:(q8 + 1) * 128], mb[:, qt, :], c.ident[:, :], waits=[t3, bbf] + c.cst, inc=(q8 == 7))
            tcp = p.op("dve", lambda e, half=half, pb=pb: e.tensor_copy(out=maskT[0:16, half * 1024:(half + 1) * 1024], in_=pb[0:16, 0:1024]),
                       waits=[ttr, mask_free], inc=True)
            p.bbfree[bb] = tcp
            mb_free = ttr
        mask_ready = tcp
        last_pe = None
        for g in range(4):
            o = g % 2
            tiles = []
            for n in range(8):
                for i in range(2):
                    tiles.append((n * 256 + i * 128, n, False))
            for m in range(2 * g + 2):
                for i in range(2):
                    tiles.append((2048 + m * 256 + i * 128, 8 + m, m == 2 * g + 1))
            nt = len(tiles)

            def front(ti, g=g, o=o, tiles=tiles):
                koff, nrow, lastblk = tiles[ti]
                q0 = 256 if lastblk else 0
                qs_ = slice(g * 512 + q0, (g + 1) * 512)
                cs_ = slice(q0, 512)
                E0 = (2048 + 512 * g - koff) // 128
                b, st, bf = p.bank()
                p.mm(st[:, cs_], kbuf[s][:, koff:koff + 128], Q[:, h, qs_], True, lastblk, waits=[tk, bf, c.q_ready])
                if not lastblk:
                    tst = p.mm(st[:, cs_], c.esel[:, nrow, :], maskT[:, qs_], False, True,
                               waits=[mask_ready] + c.cst2, inc=True)
                else:
                    tst = _pe_token(p)
                ps_, pt_, _, pfree = ptr.next()
                if E0 <= 7:
                    e0 = E0 + 3 + q0 // 128
                    ts_, tmp, _, tfree = c.tmpring.next()
                    ta = p.tt(tmp[:, cs_], st[:, cs_], toep[:, e0 * 128:e0 * 128 + (512 - q0)], ALU.add, waits=[tst, tt_, tfree])
                    p.bfree[b] = ta
                    te = p.act(pt_[:, cs_], tmp[:, cs_], AF.Exp, waits=[ta, pfree, pt_free2[ps_]])
                    c.tmpring.free[ts_] = te
                else:
                    te = p.act(pt_[:, cs_], st[:, cs_], AF.Exp, waits=[tst, pfree, pt_free2[ps_]] + c.cst2, bias=c.b31[:, h:h + 1])
                    p.bfree[b] = te
                return (ps_, pt_, te, cs_)

            def back(ti, info, g=g, o=o, tiles=tiles, nt=nt):
                ps_, pt_, te, cs_ = info
                koff = tiles[ti][0]
                lp = p.mm(OT[o][:, cs_], vbuf[s][:, koff // 128, :], pt_[:, cs_], ti == 0, ti == nt - 1,
                          waits=[te, tv, od_free[o]], inc=True)
                ptr.free[ps_] = lp
                if ti == 0:
                    tp_ = p.op("pool", lambda e, pt_=pt_, o=o: e.tensor_copy(out=accP[o][:, :], in_=pt_[:, :]), waits=[te, acc_free[o]], inc=True)
                else:
                    tp_ = p.tt(accP[o][:, cs_], accP[o][:, cs_], pt_[:, cs_], ALU.add, waits=[te, acc_tok[o]], eng="pool")
                acc_tok[o] = tp_
                pt_free2[ps_] = tp_
                if ti == nt - 1:
                    lp = p.mm(DEN[o][:, :], ones32, accP[o][:, :], True, True, waits=[tp_, t_ones32, od_free[o]], inc=True)
                return lp

            DEPTH = 2
            infos = {}
            for ti in range(nt + DEPTH):
                if ti < nt:
                    infos[ti] = front(ti)
                if ti - DEPTH >= 0:
                    last_pe = back(ti - DEPTH, infos.pop(ti - DEPTH))
            tr_ = p.op("dve", lambda e, o=o: e.reciprocal(out=accP[o][:, :], in_=DEN[o][:, :]), waits=[last_pe], inc=True)
            to = p.tt(A[:, h, g * 512:(g + 1) * 512], OT[o][:, :], accP[o][:, :], ALU.mult, waits=[tr_])
            od_free[o] = to
            acc_free[o] = to
            c.a_last = to
        kfree[s] = last_pe
        toep_free = _dve_token(p)
        mask_free = last_pe
    p.nrot = p.nfb


def _dve_token(p):
    if p.dry:
        return None
    st = p.streams["dve"]
    for ent in reversed(st):
        if ent[1] is not None:
            if ent[2] is None:
                p.cnt["m_dve"] += 1
                ent[2] = ("m_dve", 1)
            break
    return ("m_dve", p.cnt["m_dve"])


def build_l1(p, nc, dr, c=None, hT_in=None, kTa=None, va=None):
    if c is None:
        c = Ctx()
        c.ident_dram = dr["ident"]
        c.out_toks = []
        setup_common(p, nc, c, dr["gains"], 8)
    hT_in = dr["hT_in"] if hT_in is None else hT_in
    kTa = dr["kTa"] if kTa is None else kTa
    va = dr["va"] if va is None else va
    hs = dr["hs"]
    c.b_free = None
    tmpring = c.tmpring
    c.kmb = p.sb("kmb", [128, 16, 16], BF16)
    c.pastb = p.sb("pastb", [128, 16, 16], F32)
    c.b31 = p.sb("b31", [128, 16], F32)
    if hasattr(c, "decT"):
        c.esel = c.decT[:, :, :].rearrange("p a b -> p (a b)").bitcast(BF16).rearrange("p (n k) -> p n k", n=16)
    else:
        c.esel = p.sb("esel", [128, 16, 128], BF16)
    c.kmb_ready = p.dma("pool", c.kmb[:, :, :], dr["kma"], p.sem("cst2p"))
    te = p.dma("pool", c.esel[:, :, :], dr["esel"], p.sem("cst2p"))
    c.cst2 = [p.dma("sync", c.pastb[:, :, :], dr["pastb"], p.sem("cst2")),
              p.dma("sync", c.b31[:, :], dr["b31"], p.sem("cst2")), te, c.kmb_ready]
    c.a_ready = norm_phase(p, c, hT_in, [(GI_ATT, c.A)])
    c.act_ready = c.a_ready
    p.barrier()

    def q_handler(ocg, tg, b, pt, petok):
        t1 = p.act(c.B[:, ocg, tg * 512:(tg + 1) * 512], pt[:, :], AF.Copy, waits=[petok], scale=float(128 ** -0.5))
        p.bfree[b] = t1
        c.q_ready = t1

    proj_fm(p, c, c.A, dr["w_q"], 0, 16, 0, 2048, q_handler)
    p.wfence()
    p.barrier()
    attention_phase(p, c, dr, kTa, va)
    p.barrier()
    p.wopen()
    c.act_ready = c.a_last
    res = Residual(p, c, hT_in, hs)
    proj_fm(p, c, c.A, dr["w_o"], 0, 16, 0, 2048, res.post, pre=res.pre)
    p.barrier()
    c.a_ready = norm_phase(p, c, hs, [(GI_MLP1, c.A)])
    p.barrier()
    mlp_phase(p, c, hs, dr["w_up1"], dr["w_dn1"], tmpring)
    p.barrier()
    pT_sb = c.arena[:, 24576:28672].rearrange("p (k t) -> p k t", k=2)
    c.p_ready = p.dma("pool", pT_sb[:, :, :], dr["pT1"].rearrange("(kc p) t -> p kc t", p=128), p.sem("pld"))
    c.a_ready = norm_phase(p, c, hs, [(GI_PLE1, c.A)])
    ple_phase(p, c, hs, dr["pw_gate1"], dr["pw_up1"], pT_sb, tmpring)
    p.barrier()
    norm_phase(p, c, hs, [(GI_FIN, None)], out_dram=dr["outT"])
    p.barrier()
    p.wait_only("sync", c.out_toks)


def t5_bucket_np(dist):
    n = np.maximum(dist, 0)
    nf = np.maximum(n, 16).astype(np.float32)
    val = (np.log(nf / np.float32(16)) / np.float32(math.log(1024 / 16))).astype(np.float32) * np.float32(16)
    large = 16 + val.astype(np.int32)
    large = np.minimum(large, 31)
    return np.where(n < 16, n, large)


def l1_consts(rel_bias, hf):
    ki = np.arange(128)[:, None, None]
    ei = np.arange(14)[None, :, None]
    qi = np.arange(128)[None, None, :]
    dist = 128 * (ei - 3) + qi - ki
    bucket = t5_bucket_np(dist)
    toep = np.empty((16, 128, 14 * 128), np.float32)
    for h in range(16):
        toep[h] = np.where(dist >= 0, rel_bias[:, h][bucket], np.float32(NEG)).reshape(128, 14 * 128)
    b31 = np.ascontiguousarray(np.broadcast_to(rel_bias[31][None, :], (128, 16))).astype(np.float32)
    esel = np.zeros((128, 16, 128), np.float32)
    for n in range(16):
        esel[n, n, :] = 1.0
    pastb = np.full((128, 16, 16), NEG, np.float32)
    for qt in range(16):
        j = qt // 2
        if hf == 1:
            pastb[:, qt, 0:8] = 0.0
        pastb[:, qt, 8:8 + j] = 0.0
    return toep, b31, esel, pastb


def kernel(**inputs):
    inputs = {k: np.asarray(v) for k, v in inputs.items()}
    x = inputs["x"]
    pp = inputs["p"]
    nc, _ = build_nc_fused()
    zx, decT = ret_consts()
    gains = gain_layout([inputs["ret_norm_g"][0], inputs["mlp_norm_g"][0], inputs["ple_norm_g"][0], inputs["kv_norm_g"],
                         inputs["att_norm_g"][0], inputs["mlp_norm_g"][1], inputs["ple_norm_g"][1], inputs["final_norm_g"]])
    ident = np.eye(128, dtype=np.float32)
    cs = [rot_tables(0), rot_tables(T)]
    ca = np.ascontiguousarray
    shared = dict(gains=gains, ident=ident, zx=zx, decT=decT,
                  w_in=ca(inputs["ret_w_in"][0]), w_out=ca(inputs["ret_w_out"][0]),
                  w_up=ca(inputs["mlp_w_up"][0]), w_dn=ca(inputs["mlp_w_down"][0]),
                  pw_up=ca(inputs["ple_w_up"][0]), pw_gate=ca(inputs["ple_w_gate"][0]), w_kv=ca(inputs["w_kv"]),
                  w_q=ca(inputs["att_w_q"][0]), w_o=ca(inputs["att_w_o"][0]),
                  w_up1=ca(inputs["mlp_w_up"][1]), w_dn1=ca(inputs["mlp_w_down"][1]),
                  pw_up1=ca(inputs["ple_w_up"][1]), pw_gate1=ca(inputs["ple_w_gate"][1]))
    csts = [l1_consts(np.asarray(inputs["rel_bias"], np.float32), hf) for hf in range(2)]
    in_maps = []
    for cid in range(8):
        b, hf = cid // 2, cid % 2
        m = dict(shared)
        toep, b31, esel, pastb = csts[hf]
        m.update(toep=toep, b31=b31, esel=esel, pastb=pastb)
        own = ca(x[b, hf * T:(hf + 1) * T, :].T)
        p_own = ca(pp[0, b, hf * T:(hf + 1) * T, :].T)
        if hf == 1:
            prev = ca(x[b, 0:T, :].T)
            p_prev = ca(pp[0, b, 0:T, :].T)
        else:
            prev = np.zeros_like(own)
            p_prev = np.zeros_like(p_own)
        m["xT2"] = np.stack([prev, own])
        m["pT2"] = np.stack([p_prev, p_own])
        m["cs2"] = np.stack([cs[0], cs[hf]])
        m["pT1"] = ca(pp[1, b, hf * T:(hf + 1) * T, :].T)
        in_maps.append(m)
    res = run_bass_kernel_spmd(nc, in_maps, core_ids=list(range(8)))
    out = np.empty((4, 4096, D), np.float32)
    for cid in range(8):
        b, hf = cid // 2, cid % 2
        out[b, hf * T:(hf + 1) * T, :] = np.asarray(res.results[cid]["outT"]).T
    return out
```

```python
import math
import numpy as np
from contextlib import ExitStack
import concourse.bass as bass
import concourse.mybir as mybir
from concourse.bass_utils import run_bass_kernel_spmd

F32 = mybir.dt.float32
BF16 = mybir.dt.bfloat16
AF = mybir.ActivationFunctionType
ALU = mybir.AluOpType

ENGS = ("sync", "act", "dve", "pool", "pe")
T = 2048
D = 2048
NFB = 6


class Dummy:
    def __getitem__(self, k):
        return self

    def __getattr__(self, k):
        return self

    def __call__(self, *a, **k):
        return self


class Prog:
    def __init__(self, nc, dry=False, wplan=None, nfb=6, nbb=2, fences=None):
        self.nc = nc
        self.dry = dry
        self.es = ExitStack()
        self.streams = {e: [] for e in ENGS}
        self.semh = {}
        self.cnt = {}
        self.waited = {e: {} for e in ENGS}
        for e in ENGS:
            self.sem("m_" + e)
        self.wplan = wplan if wplan is not None else []
        self.wi = 0
        self.wissued = 0
        self.wrel = {}
        self.wtok = {}
        self.WS = 3
        self.wslots = [self.sb(f"wslot{i}", [128, 16, 512], BF16) for i in range(self.WS)]
        for i in range(self.WS):
            self.sem(f"wl{i}")
        self.nfb = nfb
        self.nbb = nbb
        self.nrot = nfb
        self.fences = fences if fences is not None else []
        self.opened = set()
        self.psf = [self.ps(f"psf{i}", [128, 512], F32) for i in range(nfb)]
        self.psb = [self.ps(f"psb{i}", [128, 1024], BF16) for i in range(nbb)]
        self.bfree = [None] * nfb
        self.bbfree = [None] * nbb
        self.bi = 0
        self.bbi = 0

    def sem(self, key):
        if key not in self.semh:
            self.semh[key] = None if self.dry else self.es.enter_context(self.nc.semaphore(key))
            self.cnt[key] = 0
        return key

    def sb(self, name, shape, dt):
        if self.dry:
            return Dummy()
        return self.es.enter_context(self.nc.sbuf_tensor("s_" + name, shape, dt))

    def ps(self, name, shape, dt=F32):
        if self.dry:
            return Dummy()
        return self.es.enter_context(self.nc.psum_tensor("p_" + name, shape, dt))

    def bank(self):
        b = self.bi % self.nrot
        self.bi += 1
        return b, self.psf[b], self.bfree[b]

    def bbank(self):
        b = self.bbi % self.nbb
        self.bbi += 1
        return b, self.psb[b], self.bbfree[b]

    def _waits(self, eng, waits):
        out = []
        w = self.waited[eng]
        for tok in waits:
            if tok is None:
                continue
            key, val = tok
            if w.get(key, 0) >= val:
                continue
            w[key] = val
            out.append((key, val))
        return out

    def op(self, eng, fn, waits=(), inc=False):
        if self.dry:
            return ("m_" + eng, 0)
        ws = self._waits(eng, waits)
        tok = None
        incd = None
        if inc:
            key = "m_" + eng
            self.cnt[key] += 1
            tok = (key, self.cnt[key])
            incd = (key, 1)
        self.streams[eng].append([ws, fn, incd])
        return tok

    def dma(self, eng, out, in_, semkey, waits=()):
        if self.dry:
            return (semkey, 0)
        self.sem(semkey)
        ws = self._waits(eng, waits)
        self.cnt[semkey] += 16
        tok = (semkey, self.cnt[semkey])
        self.streams[eng].append([ws, lambda e: e.dma_start(out=out, in_=in_), (semkey, 16)])
        return tok

    def wait_only(self, eng, waits):
        if self.dry:
            return
        ws = self._waits(eng, waits)
        if ws:
            self.streams[eng].append([ws, None, None])

    def barrier(self):
        if self.dry:
            return
        toks = []
        for e in ("act", "dve", "pe"):
            st = self.streams[e]
            for ent in reversed(st):
                if ent[1] is not None:
                    if ent[2] is None:
                        self.cnt["m_" + e] += 1
                        ent[2] = ("m_" + e, 1)
                    break
            toks.append(("m_" + e, self.cnt["m_" + e]))
        for key, c in self.cnt.items():
            if not key.startswith("m_") and c > 0:
                toks.append((key, c))
        for e in ENGS:
            self.wait_only(e, toks)

    def wnext(self, src, kc, ncols):
        if self.dry:
            self.wplan.append((src, kc, ncols))
            return Dummy(), None, len(self.wplan) - 1
        i = self.wi
        self.wi += 1
        assert self.wplan[i][1] == kc and self.wplan[i][2] == ncols
        self.wprefetch(i + self.WS - 1)
        return self.wslots[i % self.WS], self.wtok[i], i

    def wfence(self):
        if self.dry:
            self.fences.append(len(self.wplan))

    def wopen(self):
        if not self.dry:
            self.opened.add(self.wi)

    def wprefetch(self, j):
        if self.dry:
            return
        while self.wissued <= min(j, len(self.wplan) - 1):
            s = self.wissued
            if s in self.fences and s not in self.opened:
                break
            slot = s % self.WS
            src, kc, ncols = self.wplan[s]
            waits = []
            if s - self.WS >= 0:
                if (s - self.WS) not in self.wrel:
                    assert s > j - (self.WS - 1), "needed slab cannot be issued"
                    break
                waits.append(self.wrel[s - self.WS])
            self.wtok[s] = self.dma("pool", self.wslots[slot][:, 0:kc, 0:ncols], src, f"wl{slot}", waits)
            self.wissued += 1

    def wrelease(self, i, tok):
        if self.dry:
            return
        self.wrel[i] = tok

    def mm(self, out, lhsT, rhs, start, stop, waits=(), inc=False):
        return self.op("pe", lambda e: e.matmul(out, lhsT=lhsT, rhs=rhs, start=start, stop=stop), waits, inc)

    def tr(self, out, in_, ident, waits=(), inc=False):
        return self.op("pe", lambda e: e.transpose(out, in_, ident), waits, inc)

    def act(self, out, in_, func, waits=(), inc=True, **kw):
        return self.op("act", lambda e: e.activation(out=out, in_=in_, func=func, **kw), waits, inc)

    def tt(self, out, in0, in1, op, waits=(), inc=True, eng="dve"):
        return self.op(eng, lambda e: e.tensor_tensor(out=out, in0=in0, in1=in1, op=op), waits, inc)

    def ts(self, out, in0, s1, s2, op0, op1=None, waits=(), inc=True, eng="dve"):
        if op1 is None:
            return self.op(eng, lambda e: e.tensor_scalar(out=out, in0=in0, scalar1=s1, scalar2=None, op0=op0), waits, inc)
        return self.op(eng, lambda e: e.tensor_scalar(out=out, in0=in0, scalar1=s1, scalar2=s2, op0=op0, op1=op1), waits, inc)

    def stt(self, out, in0, scalar, in1, op0, op1, waits=(), inc=True, eng="dve"):
        return self.op(eng, lambda e: e.scalar_tensor_tensor(out=out, in0=in0, scalar=scalar, in1=in1, op0=op0, op1=op1), waits, inc)

    def emit(self):
        nc = self.nc
        with nc.Block() as block:
            def run(stream):
                def f(e):
                    for ws, fn, inc in stream:
                        for key, val in ws:
                            e.wait_ge(self.semh[key], val)
                        if fn is None:
                            continue
                        ins = fn(e)
                        if inc is not None:
                            ins.then_inc(self.semh[inc[0]], inc[1])
                return f
            block.sync(run(self.streams["sync"]))
            block.scalar(run(self.streams["act"]))
            block.vector(run(self.streams["dve"]))
            block.gpsimd(run(self.streams["pool"]))
            block.tensor(run(self.streams["pe"]))
        self.es.close()


def wslab(W, r0, kc, c0, ncols=512):
    return W[r0:r0 + kc * 128, c0:c0 + ncols].rearrange("(kc p) n -> p kc n", p=128)


class Ring:
    def __init__(self, p, name, n, shape, dt, tiles=None):
        self.p = p
        self.n = n
        self.tiles = tiles if tiles is not None else [p.sb(f"{name}{i}", shape, dt) for i in range(n)]
        self.sems = [p.sem(f"{name}_s{i}") for i in range(n)]
        self.free = [None] * n
        self.i = 0

    def next(self):
        s = self.i % self.n
        self.i += 1
        return s, self.tiles[s], self.sems[s], self.free[s]


class Ctx:
    pass


def setup_common(p, nc, c, gdram, n_g):
    c.A = p.sb("A", [128, 16, T], BF16)
    c.arena = p.sb("B", [128, 16 * T], BF16)
    ar = c.arena
    c.B = ar[:, :].rearrange("p (k t) -> p k t", k=16)
    c.g = p.sb("gains", [128, n_g, 16], F32)
    c.ones = p.sb("ones", [128, 128], BF16)
    c.ident = p.sb("ident", [128, 128], BF16)
    c.hring = Ring(p, "hr", 2, None, None, tiles=[ar[:, i * 4096:(i + 1) * 4096].bitcast(F32) for i in range(2)])
    c.sqring = Ring(p, "sq", 2, None, None, tiles=[ar[:, 8192 + i * 2048:8192 + (i + 1) * 2048] for i in range(2)])
    c.rstd = ar[:, 12288:16384].bitcast(F32)
    c.rring = Ring(p, "rr", 3, [128, 512], F32)
    c.tmpring = Ring(p, "tmp", 4, [128, 512], F32)
    c.oring = Ring(p, "obf", 4, [128, 512], BF16)
    c.htok = {}
    t1 = p.dma("sync", c.g[:, :, :], gdram, p.sem("cst"))
    t2 = p.dma("pool", c.ident[:, :], c.ident_dram, p.sem("cst"))
    c.eps = p.sb("eps", [128, 2], F32)
    p.op("dve", lambda e: e.memset(c.eps[:, 0:1], 1e-6))
    p.op("dve", lambda e: e.memset(c.eps[:, 1:2], 1e-5))
    t3 = p.op("dve", lambda e: e.memset(c.ones[:, :], 1.0), inc=True)
    c.cst = [t1, t2, t3]


def norm_phase(p, c, hT, gi_dsts, out_dram=None):
    banks = [p.bank() for _ in range(4)]
    gi, dst = gi_dsts[0]
    last_mm = None
    thg = None
    for ch in range(16):
        s, ht, hs, hfree = c.hring.next()
        tl = p.dma("sync", ht[:, :], hT[ch * 128:(ch + 1) * 128, :], hs, waits=[hfree])
        s2, sq, _, sqfree = c.sqring.next()
        tsq = p.act(sq[:, :], ht[:, :], AF.Square, waits=[tl, sqfree])
        if out_dram is None:
            thg = p.ts(dst[:, ch, :], ht[:, :], c.g[:, gi, ch:ch + 1], None, ALU.mult, waits=[tl, tsq] + c.cst)
            c.hring.free[s] = thg
        else:
            c.hring.free[s] = tsq
        for tg in range(4):
            b, pt, bf = banks[tg]
            last_mm = p.mm(pt[:, :], c.ones[:, :], sq[:, tg * 512:(tg + 1) * 512], ch == 0, ch == 15,
                           waits=[tsq, bf] + c.cst, inc=(tg == 3))
        c.sqring.free[s2] = last_mm
    tr = None
    for tg in range(4):
        b, pt, bf = banks[tg]
        rs_ = c.rstd[:, tg * 512:(tg + 1) * 512]
        tsq_ = p.act(rs_, pt[:, :], AF.Sqrt, waits=[last_mm] + c.cst, scale=1.0 / D, bias=c.eps[:, 0:1])
        p.bfree[b] = tsq_
        tr = p.op("dve", lambda e, rs_=rs_: e.reciprocal(out=rs_, in_=rs_), waits=[tsq_], inc=True)
    last = None
    if out_dram is None:
        for ch in range(16):
            last = p.tt(dst[:, ch, :], dst[:, ch, :], c.rstd[:, :], ALU.mult, waits=[tr, thg])
        return last
    for ch in range(16):
        s, ht, hs, hfree = c.hring.next()
        tl = p.dma("sync", ht[:, :], hT[ch * 128:(ch + 1) * 128, :], hs, waits=[hfree])
        last = p.stt(ht[:, :], ht[:, :], c.g[:, gi, ch:ch + 1], c.rstd[:, :], ALU.mult, ALU.mult,
                     waits=[tl, tr] + c.cst)
        tst = p.dma("sync", out_dram[ch * 128:(ch + 1) * 128, :], ht[:, :], hs, waits=[last])
        c.hring.free[s] = tst
        c.out_toks.append(tst)
    return last


def proj_fm(p, c, act, W, r0, kc, c0, ncols_total, handler, pre=None, order="tg_oc"):
    nslab = ncols_total // 512
    for s in range(nslab):
        wt, wtok, wi = p.wnext(wslab(W, r0, kc, c0 + s * 512), kc, 512)
        last = None
        for tg in range(4):
            for oc in range(4):
                ocg = s * 4 + oc
                if pre is not None:
                    pre(ocg, tg)
                b, pt, bf = p.bank()
                for k in range(kc):
                    last = p.mm(pt[:, :], wt[:, k, oc * 128:(oc + 1) * 128], act[:, k, tg * 512:(tg + 1) * 512],
                                k == 0, k == kc - 1, waits=[wtok, bf, c.act_ready] if k == 0 else (), inc=(k == kc - 1))
                handler(ocg, tg, b, pt, last)
        p.wrelease(wi, last)


def proj_tm(p, c, act, W, r0, kc, c0, ncols_total, handler):
    nslab = ncols_total // 512
    for s in range(nslab):
        wt, wtok, wi = p.wnext(wslab(W, r0, kc, c0 + s * 512), kc, 512)
        last = None
        for tt in range(16):
            b, pt, bf = p.bank()
            for k in range(kc):
                last = p.mm(pt[:, :], act[:, k, tt * 128:(tt + 1) * 128], wt[:, k, :],
                            k == 0, k == kc - 1, waits=[wtok, bf, c.act_ready] if k == 0 else (), inc=(k == kc - 1))
            handler(s, tt, b, pt, last)
        p.wrelease(wi, last)


class Residual:
    def __init__(self, p, c, src, dst):
        self.p, self.c, self.src, self.dst = p, c, src, dst
        self.pending = {}

    def pre(self, ocg, tg):
        p, c = self.p, self.c
        s, rt, rs, rfree = c.rring.next()
        tl = p.dma("sync", rt[:, :], self.src[ocg * 128:(ocg + 1) * 128, tg * 512:(tg + 1) * 512], rs,
                   waits=[rfree, c.htok.get((ocg, tg))])
        self.pending[(ocg, tg)] = (s, rt, rs, tl)

    def post(self, ocg, tg, b, pt, petok):
        p, c = self.p, self.c
        s, rt, rs, tl = self.pending.pop((ocg, tg))
        ta = p.tt(rt[:, :], pt[:, :], rt[:, :], ALU.add, waits=[petok, tl])
        p.bfree[b] = ta
        tst = p.dma("sync", self.dst[ocg * 128:(ocg + 1) * 128, tg * 512:(tg + 1) * 512], rt[:, :], rs, waits=[ta])
        c.rring.free[s] = tst
        c.htok[(ocg, tg)] = tst


def mlp_phase(p, c, hT, W1, W2, tmpring):
    for q in range(4):
        def up_handler(ocg, tg, b, pt, petok):
            s, tt_, _, tfree = tmpring.next()
            t1 = p.act(tt_[:, :], pt[:, :], AF.Relu, waits=[petok, tfree])
            p.bfree[b] = t1
            t2 = p.tt(c.B[:, ocg, tg * 512:(tg + 1) * 512], tt_[:, :], tt_[:, :], ALU.mult, waits=[t1, c.b_free])
            tmpring.free[s] = t2
            c.b_last = t2
        c.act_ready = c.a_ready
        proj_fm(p, c, c.A, W1, 0, 16, q * 2048, 2048, up_handler)
        c.act_ready = c.b_last
        res = Residual(p, c, hT, hT)
        proj_fm(p, c, c.B, W2, q * 2048, 16, 0, 2048, res.post, pre=res.pre)
        c.b_free = _pe_token(p)


def _pe_token(p):
    if p.dry:
        return None
    st = p.streams["pe"]
    for ent in reversed(st):
        if ent[1] is not None:
            if ent[2] is None:
                p.cnt["m_pe"] += 1
                ent[2] = ("m_pe", 1)
            break
    return ("m_pe", p.cnt["m_pe"])


def ple_phase(p, c, hT, Wg, Wu, pT_sb, tmpring):
    for s in range(4):
        wg, wgtok, wgi = p.wnext(wslab(Wg, 0, 16, s * 512), 16, 512)
        wu, wutok, wui = p.wnext(wslab(Wu, 0, 2, s * 512), 2, 512)
        last = None
        for tg in range(4):
            for oc in range(4):
                ocg = s * 4 + oc
                sr, rt, rs, rfree = c.rring.next()
                tl = p.dma("sync", rt[:, :], hT[ocg * 128:(ocg + 1) * 128, tg * 512:(tg + 1) * 512], rs,
                           waits=[rfree, c.htok.get((ocg, tg))])
                b, pt, bf = p.bank()
                for k in range(16):
                    last = p.mm(pt[:, :], wg[:, k, oc * 128:(oc + 1) * 128], c.A[:, k, tg * 512:(tg + 1) * 512],
                                k == 0, k == 15, waits=[wgtok, bf, c.a_ready] if k == 0 else (), inc=(k == 15))
                st, gt, _, tfree = tmpring.next()
                tg_ = p.act(gt[:, :], pt[:, :], AF.Sigmoid, waits=[last, tfree])
                p.bfree[b] = tg_
                b2, pt2, bf2 = p.bank()
                for k in range(2):
                    last = p.mm(pt2[:, :], wu[:, k, oc * 128:(oc + 1) * 128], pT_sb[:, k, tg * 512:(tg + 1) * 512],
                                k == 0, k == 1, waits=[wutok, bf2, c.p_ready] if k == 0 else (), inc=(k == 1))
                tm = p.tt(gt[:, :], pt2[:, :], gt[:, :], ALU.mult, waits=[last, tg_])
                p.bfree[b2] = tm
                ta = p.tt(rt[:, :], rt[:, :], gt[:, :], ALU.add, waits=[tl, tm])
                tmpring.free[st] = ta
                tst = p.dma("sync", hT[ocg * 128:(ocg + 1) * 128, tg * 512:(tg + 1) * 512], rt[:, :], rs, waits=[ta])
                c.rring.free[sr] = tst
                c.htok[(ocg, tg)] = tst
        p.wrelease(wgi, last)
        p.wrelease(wui, last)


GI_RET, GI_MLP0, GI_PLE0, GI_KV, GI_ATT, GI_MLP1, GI_PLE1, GI_FIN = range(8)


def l0_setup(p, c, dr):
    c.consts = p.sb("consts", [128, 16], F32)
    c.decT = p.sb("decT", [128, 8, 128], F32)
    tcst = [p.dma("sync", c.consts[:, :], dr["zx"], p.sem("cst")),
            p.dma("sync", c.decT[:, :, :], dr["decT"], p.sem("cst"))]
    c.cst = c.cst + tcst
    c.kzr = Ring(p, "kz", 2, [128, 256], BF16)
    c.sTr = Ring(p, "sT", 2, [128, 128], BF16)
    c.stat = Ring(p, "stat", 2, [128, 16], F32)
    c.km = p.sb("km", [128, 16, 8], F32)


def l0_half(p, c, dr, half):
    xT = dr["xT2"][half]
    hT = dr["hT2"][half]
    c.htok = {}
    tmpring = c.tmpring
    oring = c.oring
    consts, decT = c.consts, c.decT
    cs_cos, cs_sin = c.hring.tiles[0], c.hring.tiles[1]
    pT_sb = c.arena[:, 24576:28672].rearrange("p (k t) -> p k t", k=2)
    c.b_free = None
    qT_s, kT_s, v_s, sg_s = dr["qT_s"], dr["kT_s"], dr["v_s"], dr["sg_s"]
    W_in = dr["w_in"]

    def rot_handler(dst):
        state = {}

        def h(ocg, tg, b, pt, petok):
            if ocg % 2 == 0:
                state[tg] = (b, pt, petok)
                return
            b1, x1, tk1 = state.pop(tg)
            b2, x2, tk2 = b, pt, petok
            cos = cs_cos[:, tg * 512:(tg + 1) * 512]
            sin = cs_sin[:, tg * 512:(tg + 1) * 512]
            s1, ta, _, f1 = tmpring.next()
            s2, tb, _, f2 = tmpring.next()
            so1, o1, os1, of1 = oring.next()
            so2, o2, os2, of2 = oring.next()
            w = [tk1, tk2, c.cs_ready]
            p.tt(ta[:, :], x1[:, :], cos, ALU.mult, waits=w + [f1])
            p.tt(tb[:, :], x2[:, :], sin, ALU.mult, waits=[f2])
            t_o1 = p.tt(o1[:, :], ta[:, :], tb[:, :], ALU.subtract, waits=[of1])
            p.tt(ta[:, :], x1[:, :], sin, ALU.mult)
            t4 = p.tt(tb[:, :], x2[:, :], cos, ALU.mult)
            p.bfree[b1] = t4
            p.bfree[b2] = t4
            t_o2 = p.tt(o2[:, :], ta[:, :], tb[:, :], ALU.add, waits=[of2])
            tmpring.free[s1] = t_o2
            tmpring.free[s2] = t_o2
            r1 = (ocg - 1) * 128
            oring.free[so1] = p.dma("sync", dst[r1:r1 + 128, tg * 512:(tg + 1) * 512], o1[:, :], os1, waits=[t_o1])
            oring.free[so2] = p.dma("sync", dst[r1 + 128:r1 + 256, tg * 512:(tg + 1) * 512], o2[:, :], os2, waits=[t_o2])
        return h

    def tm_handler(dst, func):
        def h(s, tt, b, pt, petok):
            so, o, osm, of = oring.next()
            t1 = p.act(o[:, :], pt[:, :], func, waits=[petok, of])
            p.bfree[b] = t1
            oring.free[so] = p.dma("sync", dst[tt * 128:(tt + 1) * 128, s * 512:(s + 1) * 512], o[:, :], osm, waits=[t1])
        return h

    c.a_ready = norm_phase(p, c, xT, [(GI_RET, c.A)])
    c.act_ready = c.a_ready
    p.barrier()
    p.dma("sync", cs_cos[:, :], dr["cs2"][half, 0], p.sem("csl"))
    c.cs_ready = p.dma("sync", cs_sin[:, :], dr["cs2"][half, 1], p.sem("csl"))
    proj_fm(p, c, c.A, W_in, 0, 16, 0, 2048, rot_handler(qT_s))
    proj_fm(p, c, c.A, W_in, 0, 16, 2048, 2048, rot_handler(kT_s))
    proj_tm(p, c, c.A, W_in, 0, 16, 4096, 4096, tm_handler(v_s, AF.Copy))
    proj_tm(p, c, c.A, W_in, 0, 16, 8192, 4096, tm_handler(sg_s, AF.Silu))
    p.barrier()
    ar = c.arena
    S32f = ar[:, 0:8192].bitcast(F32)
    S32 = S32f.rearrange("p (j d e) -> p j d e", j=4, d=2)
    Sbf2 = ar[:, 8192:12288]
    Sbf = Sbf2.rearrange("p (j d e) -> p j d e", j=4, d=2)
    o = 12288
    kring = Ring(p, "kch", 3, None, None, tiles=[ar[:, o + i * 1024:o + (i + 1) * 1024].rearrange("p (f t) -> p f t", f=8) for i in range(3)])
    o += 3072
    qring = Ring(p, "qch", 3, None, None, tiles=[ar[:, o + i * 1024:o + (i + 1) * 1024].rearrange("p (f t) -> p f t", f=8) for i in range(3)])
    o += 3072
    vring = Ring(p, "vch", 3, None, None, tiles=[ar[:, o + i * 2048:o + (i + 1) * 2048] for i in range(3)])
    o += 6144
    gring = Ring(p, "gch", 3, None, None, tiles=[ar[:, o + i * 2048:o + (i + 1) * 2048] for i in range(3)])
    kzr, sTr, stat = c.kzr, c.sTr, c.stat
    yr = tmpring
    ygr = oring
    gam = [1.0 - 2.0 ** (-5.0 - h) for h in range(8)]
    pbk = p.psb[0]
    pby = p.psf[6][:, :].bitcast(BF16)
    bfk = [None]
    bfy = [None]
    p.nrot = 6
    for hp in range(2):
        h0 = hp * 4
        if half == 0:
            tz = p.op("dve", lambda e: e.memset(S32f[:, :], 0.0), inc=True)
            tzb = p.op("dve", lambda e: e.memset(Sbf2[:, :], 0.0), inc=True)
        else:
            tld = p.dma("sync", S32f[:, :], dr["Sst"][hp], p.sem("sst"))
            tz = tld
            tzb = p.act(Sbf2[:, :], S32f[:, :], AF.Copy, waits=[tld])
        sb_tok = [tzb] * 4
        s32_tok = [tz] * 4
        inter_tok = [None] * 4
        a_last = None
        for n in range(16):
            no = n
            ks, kt, ksem, kfree = kring.next()
            vs, vt, vsem, vfree = vring.next()
            qs, qt, qsem, qfree = qring.next()
            gs, gt, gsem, gfree = gring.next()
            tk = p.dma("sync", kt[:, :, :], kT_s[h0 * 256:h0 * 256 + 1024, n * 128:(n + 1) * 128].rearrange("(f p) t -> p f t", p=128),
                       ksem, waits=[kfree])
            tv = p.dma("sync", vt[:, :], v_s[n * 128:(n + 1) * 128, h0 * 512:h0 * 512 + 2048], vsem, waits=[vfree])
            tq = p.dma("sync", qt[:, :, :], qT_s[h0 * 256:h0 * 256 + 1024, n * 128:(n + 1) * 128].rearrange("(f p) t -> p f t", p=128),
                       qsem, waits=[qfree])
            tgl = p.dma("sync", gt[:, :], sg_s[n * 128:(n + 1) * 128, h0 * 512:h0 * 512 + 2048], gsem, waits=[gfree])
            lastg = None
            pendB = None
            for j in range(4):
                h = h0 + j
                pb = pbk
                p.tr(pb[:, 0:128], kt[:, 2 * j, :], c.ident[:, :], waits=[tk, bfk[0]] + c.cst)
                ttr = p.tr(pb[:, 128:256], kt[:, 2 * j + 1, :], c.ident[:, :], inc=True)
                zs, kz, _, kzfree = kzr.next()
                tkz = p.act(kz[:, :], pb[:, 0:256], AF.Copy, waits=[ttr, kzfree], scale=consts[:, h:h + 1])
                bfk[0] = tkz
                b1, ps_s, bf1 = p.bank()
                p.mm(ps_s[:, 0:128], kt[:, 2 * j, :], qt[:, 2 * j, :], True, False, waits=[tq, bf1])
                tsc = p.mm(ps_s[:, 0:128], kt[:, 2 * j + 1, :], qt[:, 2 * j + 1, :], False, True, inc=True)
                ss, sT, _, sTfree = sTr.next()
                tsT = p.tt(sT[:, :], ps_s[:, 0:128], decT[:, h, :], ALU.mult, waits=[tsc, sTfree])
                p.bfree[b1] = tsT
                b3, ps_x, bf3 = p.bank()
                p.mm(ps_x[:, :], qt[:, 2 * j, :], Sbf[:, j, 0, :], True, False, waits=[sb_tok[j], bf3])
                tix = p.mm(ps_x[:, :], qt[:, 2 * j + 1, :], Sbf[:, j, 1, :], False, True, inc=True)
                inter_tok[j] = tix
                b2, ps_i, bf2 = p.bank()
                tin = p.mm(ps_i[:, :], sT[:, :], vt[:, j * 512:(j + 1) * 512], True, True, waits=[tsT, tv, bf2], inc=True)
                sTr.free[ss] = tin
                ys, y, _, yfree = yr.next()
                ygs, yg, _, ygfree = ygr.next()
                sts, stt_, _, stfree = stat.next()
                t_x = p.act(y[:, :], ps_x[:, :], AF.Copy, waits=[tix, yfree], scale=consts[:, 8 + h:9 + h])
                p.bfree[b3] = t_x
                t_y = p.tt(y[:, :], ps_i[:, :], y[:, :], ALU.add, waits=[tin, t_x])
                p.bfree[b2] = t_y
                t_bs = p.op("dve", lambda e, stt_=stt_, y=y: e.bn_stats(out=stt_[:, 0:6], in_=y[:, :]), waits=[stfree], inc=True)
                t_ag = p.op("dve", lambda e, stt_=stt_: e.bn_aggr(out=stt_[:, 8:10], in_=stt_[:, 0:6]), waits=[t_bs], inc=True)
                t_sq = p.act(stt_[:, 9:10], stt_[:, 9:10], AF.Sqrt, waits=[t_ag], bias=c.eps[:, 1:2])
                tst_last = None
                st_banks = []
                for dc in range(2):
                    b4, ps_st, bf4 = p.bank()
                    tmm = p.mm(ps_st[:, :], kz[:, dc * 128:(dc + 1) * 128], vt[:, j * 512:(j + 1) * 512], True, True,
                               waits=[tkz, tv, bf4], inc=True)
                    st_banks.append((b4, ps_st, tmm))
                    tst_last = tmm
                kzr.free[zs] = tst_last
                t_rc = p.op("dve", lambda e, stt_=stt_: e.reciprocal(out=stt_[:, 9:10], in_=stt_[:, 9:10]), waits=[t_sq], inc=True)
                p.wait_only("dve", [t_rc])
                p.ts(y[:, :], y[:, :], stt_[:, 8:9], stt_[:, 9:10], ALU.subtract, ALU.mult)
                t_yg = p.tt(yg[:, :], y[:, :], gt[:, j * 512:(j + 1) * 512], ALU.mult, waits=[tgl, ygfree])
                yr.free[ys] = t_yg
                stat.free[sts] = t_yg
                lastg = t_yg
                tup = None
                for dc in range(2):
                    b4, ps_st, tmm = st_banks[dc]
                    tup = p.stt(S32[:, j, dc, :], S32[:, j, dc, :], float(gam[h] ** 128), ps_st[:, :], ALU.mult, ALU.add,
                                waits=[tmm, s32_tok[j]])
                    p.bfree[b4] = tup
                s32_tok[j] = tup
                sb_tok[j] = p.act(Sbf[:, j, :, :], S32[:, j, :, :], AF.Copy, waits=[tup, inter_tok[j]])

                def stageB(j=j, yg=yg, ygs=ygs, t_yg=t_yg):
                    ttr2 = None
                    for i in range(4):
                        ttr2 = p.tr(pby[:, i * 128:(i + 1) * 128], yg[:, i * 128:(i + 1) * 128], c.ident[:, :],
                                    waits=[t_yg, bfy[0]] if i == 0 else (), inc=(i == 3))
                    ygr.free[ygs] = ttr2
                    al = p.act(c.A[:, j * 4:(j + 1) * 4, no * 128:(no + 1) * 128],
                               pby[:, 0:512].rearrange("p (i t) -> p i t", i=4), AF.Copy, waits=[ttr2])
                    bfy[0] = al
                    return al
                if pendB is not None:
                    a_last = pendB()
                pendB = stageB
            a_last = pendB()
            kring.free[ks] = _pe_token(p)
            vring.free[vs] = kring.free[ks]
            qring.free[qs] = kring.free[ks]
            gring.free[gs] = lastg
        if half == 0:
            p.dma("sync", dr["Sst"][hp], S32f[:, :], p.sem("sst"), waits=s32_tok)
        c.act_ready = a_last
        res = Residual(p, c, xT if hp == 0 else hT, hT)
        proj_fm(p, c, c.A, dr["w_out"], hp * 2048, 16, 0, 2048, res.post, pre=res.pre)
        p.barrier()
    p.nrot = p.nfb
    c.a_ready = norm_phase(p, c, hT, [(GI_MLP0, c.A)])
    p.barrier()
    mlp_phase(p, c, hT, dr["w_up"], dr["w_dn"], tmpring)
    p.barrier()
    c.p_ready = p.dma("pool", pT_sb[:, :, :], dr["pT2"][half].rearrange("(kc p) t -> p kc t", p=128), p.sem("pld"))
    c.a_ready = norm_phase(p, c, hT, [(GI_PLE0, c.A)])
    ple_phase(p, c, hT, dr["pw_gate"], dr["pw_up"], pT_sb, tmpring)
    p.barrier()
    c.a_ready = norm_phase(p, c, hT, [(GI_KV, c.A)])
    c.act_ready = c.a_ready
    km = c.km

    def k_handler(ocg, tg, b, pt, petok):
        so, o, osm, of = oring.next()
        t1 = p.act(o[:, :], pt[:, :], AF.Copy, waits=[petok, of])
        t2 = p.op("dve", lambda e: e.tensor_reduce(out=km[:, ocg, tg * 2:tg * 2 + 2], in_=pt[:, :].rearrange("p (b t) -> p b t", b=2),
                                                    axis=mybir.AxisListType.X, op=ALU.add), waits=[petok, t1], inc=True)
        p.bfree[b] = t2
        c.km_last = t2
        oring.free[so] = p.dma("sync", dr["kTa"][ocg * 128:(ocg + 1) * 128, half * T + tg * 512:half * T + (tg + 1) * 512], o[:, :], osm, waits=[t1])

    def v_handler(s, tt, b, pt, petok):
        so, o, osm, of = oring.next()
        t1 = p.act(o[:, :], pt[:, :], AF.Copy, waits=[petok, of])
        p.bfree[b] = t1
        oring.free[so] = p.dma("sync", dr["va"][half * T + tt * 128:half * T + (tt + 1) * 128, s * 512:(s + 1) * 512], o[:, :], osm, waits=[t1])

    proj_fm(p, c, c.A, dr["w_kv"], 0, 16, 0, 2048, k_handler)
    proj_tm(p, c, c.A, dr["w_kv"], 0, 16, 2048, 2048, v_handler)
    tkm = p.ts(km[:, :, :], km[:, :, :], 1.0 / 256.0, None, ALU.mult, waits=[c.km_last])
    p.dma("sync", dr["kma"][:, :, half * 8:(half + 1) * 8], km[:, :, :], p.sem("kmst"), waits=[tkm])
    p.barrier()


def build_fused(p, nc, dr):
    c = Ctx()
    c.ident_dram = dr["ident"]
    c.out_toks = []
    setup_common(p, nc, c, dr["gains"], 8)
    l0_setup(p, c, dr)
    for half in range(2):
        l0_half(p, c, dr, half)
    build_l1(p, nc, dr, c=c, hT_in=dr["hT2"][1], kTa=dr["kTa"], va=dr["va"])


def declare_fused(nc):
    dr = {}

    def inp(name, shape, dt=F32):
        dr[name] = nc.dram_tensor(name, shape, dt, kind="ExternalInput").ap()

    def outp(name, shape, dt):
        dr[name] = nc.dram_tensor(name, shape, dt, kind="ExternalOutput").ap()

    def scr(name, shape, dt):
        dr[name] = nc.dram_tensor(name, shape, dt).ap()

    inp("xT2", [2, D, T]); inp("pT2", [2, 256, T]); inp("cs2", [2, 2, 128, T])
    inp("gains", [128, 8, 16]); inp("ident", [128, 128]); inp("zx", [128, 16]); inp("decT", [128, 8, 128])
    inp("w_in", [D, 12288]); inp("w_out", [4096, D]); inp("w_up", [D, 8192]); inp("w_dn", [8192, D])
    inp("pw_up", [256, D]); inp("pw_gate", [D, D]); inp("w_kv", [D, 4096])
    inp("pT1", [256, T])
    inp("toep", [16, 128, 14 * 128]); inp("pastb", [128, 16, 16]); inp("b31", [128, 16]); inp("esel", [128, 16, 128])
    inp("w_q", [D, D]); inp("w_o", [D, D]); inp("w_up1", [D, 8192]); inp("w_dn1", [8192, D])
    inp("pw_up1", [256, D]); inp("pw_gate1", [D, D])
    outp("outT", [D, T], F32)
    scr("hT2", [2, D, T], F32); scr("hs", [D, T], F32)
    scr("qT_s", [D, T], BF16); scr("kT_s", [D, T], BF16); scr("v_s", [T, 4096], BF16); scr("sg_s", [T, 4096], BF16)
    scr("Sst", [2, 128, 4096], F32)
    scr("kTa", [D, 2 * T], BF16); scr("va", [2 * T, D], BF16); scr("kma", [128, 16, 16], F32)
    return dr


def build_nc_fused():
    nc = bass.Bass("TRN2", target_bir_lowering=False)
    dr = declare_fused(nc)
    pd = Prog(nc, dry=True, nfb=7, nbb=1)
    build_fused(pd, nc, dr)
    p = Prog(nc, dry=False, wplan=pd.wplan, nfb=7, nbb=1, fences=pd.fences)
    build_fused(p, nc, dr)
    p.emit()
    return nc, p


def gain_layout(gs):
    return np.ascontiguousarray(np.stack([g.reshape(16, 128).T for g in gs], axis=1)).astype(np.float32)


def rot_tables(pos0):
    half = 128
    inv = (10000.0 ** (-np.arange(half, dtype=np.float32) / half)).astype(np.float32)
    pos = (pos0 + np.arange(T)).astype(np.float32)
    ang = (pos[None, :] * inv[:, None]).astype(np.float32)
    return np.stack([np.cos(ang), np.sin(ang)]).astype(np.float32)


def ret_consts():
    lg = np.log1p(-np.exp2(-5.0 - np.arange(8, dtype=np.float64)))
    idx = np.arange(128)
    zx = np.zeros((128, 16), np.float32)
    decT = np.zeros((128, 8, 128), np.float32)
    for h in range(8):
        zx[:, h] = np.exp((127 - idx) * lg[h])
        zx[:, 8 + h] = np.exp((idx + 1) * lg[h]) / 16.0
        diff = idx[None, :] - idx[:, None]
        decT[:, h, :] = np.where(diff >= 0, np.exp(np.maximum(diff, 0) * lg[h]), 0.0) / 16.0
    return zx, decT


NEG = -1.0e30


def attention_phase(p, c, dr, kTa, va):
    A, Q = c.A, c.B
    ws = p.wslots
    kbuf = [ws[0][:, 0:8, :].rearrange("p a b -> p (a b)"), ws[0][:, 8:16, :].rearrange("p a b -> p (a b)")]
    vbuf = [ws[1][:, 0:8, :].rearrange("p a (t d) -> p (a t) d", d=128), ws[1][:, 8:16, :].rearrange("p a (t d) -> p (a t) d", d=128)]
    w2 = ws[2][:, :, :].rearrange("p a b -> p (a b)")
    toep = w2[:, 0:3584].bitcast(F32)
    maskT = w2[:, 3584:5632]
    ptr = Ring(p, "pt", 4, None, None, tiles=[w2[:, 5632 + i * 512:5632 + (i + 1) * 512] for i in range(4)])
    ksem = [p.sem("kh0"), p.sem("kh1")]
    vsem = [p.sem("vh0"), p.sem("vh1")]
    tsem = p.sem("toep")
    kfree = [None, None]
    toep_free = None
    gm = p.sb("gm", [128, 16, 16], F32)
    top8 = p.sb("top8", [128, 16, 8], F32)
    thr = p.sb("thr", [128, 16], F32)
    mb = p.sb("mb", [128, 16, 16], BF16)
    OT = [p.psf[3], p.psf[4]]
    DEN = [p.psf[5], p.psf[6]]
    od_free = [None, None]
    accP = [c.rring.tiles[0], c.rring.tiles[1]]
    ones32 = c.rring.tiles[2][:, 0:128]
    p.op("dve", lambda e: e.memset(maskT[:, :], 0.0))
    t_ones32 = p.op("dve", lambda e: e.memset(ones32, 1.0), inc=True)
    acc_free = [None, None]
    acc_tok = [None, None]
    pt_free2 = [None] * 4
    p.nrot = 3
    mask_free = None
    gm_free = None
    mb_free = None
    for h in range(16):
        s = h % 2
        tk = p.dma("sync", kbuf[s][:, :], kTa[h * 128:(h + 1) * 128, :], ksem[s], waits=[kfree[s]])
        tv = p.dma("sync", vbuf[s][:, :, :], va[:, h * 128:(h + 1) * 128].rearrange("(t p) d -> p t d", p=128), vsem[s], waits=[kfree[s]])
        tt_ = p.dma("sync", toep[:, :], dr["toep"][h], tsem, waits=[toep_free])
        b, gps, bf = p.bank()
        tg = None
        for qt in range(16):
            tg = p.mm(gps[:, qt * 16:(qt + 1) * 16], Q[:, h, qt * 128:(qt + 1) * 128], c.kmb[:, h, :], True, True,
                      waits=[bf, c.q_ready, c.kmb_ready] if qt == 0 else (), inc=(qt == 15))
        t1 = p.tt(gm[:, :, :], gps[:, 0:256].rearrange("p (a b) -> p a b", a=16), c.pastb[:, :, :], ALU.add, waits=[tg, gm_free] + c.cst2)
        p.bfree[b] = t1
        tm = None
        for qt in range(16):
            tm = p.op("dve", lambda e, qt=qt: e.max(out=top8[:, qt, :], in_=gm[:, qt, :]), waits=[t1], inc=True)
        t2 = p.ts(thr[:, :], top8[:, :, 2], -1.0e29, None, ALU.max, waits=[tm])
        t3 = None
        for qt in range(16):
            t3 = p.ts(mb[:, qt, :], gm[:, qt, :], thr[:, qt:qt + 1], NEG, ALU.is_lt, ALU.mult, waits=[t2, mb_free])
        for qt in range(16):
            jj = 8 + qt // 2
            t3 = p.op("dve", lambda e, qt=qt, jj=jj: e.memset(mb[:, qt, jj:jj + 1], 0.0), waits=[t3], inc=True)
        gm_free = t3
        tcp = None
        for half in range(2):
            bb, pb, bbf = p.bbank()
            ttr = None
            for q8 in range(8):
                qt = half * 8 + q8
                ttr = p.tr(pb[0:16, q8 * 128:(q8 + 1) * 128], mb[:, qt, :], c.ident[:, :], waits=[t3, bbf] + c.cst, inc=(q8 == 7))
            tcp = p.op("dve", lambda e, half=half, pb=pb: e.tensor_copy(out=maskT[0:16, half * 1024:(half + 1) * 1024], in_=pb[0:16, 0:1024]),
                       waits=[ttr, mask_free], inc=True)
            p.bbfree[bb] = tcp
            mb_free = ttr
        mask_ready = tcp
        last_pe = None
        for g in range(4):
            o = g % 2
            tiles = []
            for n in range(8):
                for i in range(2):
                    tiles.append((n * 256 + i * 128, n, False))
            for m in range(2 * g + 2):
                for i in range(2):
                    tiles.append((2048 + m * 256 + i * 128, 8 + m, m == 2 * g + 1))
            nt = len(tiles)

            def front(ti, g=g, o=o, tiles=tiles):
                koff, nrow, lastblk = tiles[ti]
                q0 = 256 if lastblk else 0
                qs_ = slice(g * 512 + q0, (g + 1) * 512)
                cs_ = slice(q0, 512)
                E0 = (2048 + 512 * g - koff) // 128
                b, st, bf = p.bank()
                p.mm(st[:, cs_], kbuf[s][:, koff:koff + 128], Q[:, h, qs_], True, lastblk, waits=[tk, bf, c.q_ready])
                if not lastblk:
                    tst = p.mm(st[:, cs_], c.esel[:, nrow, :], maskT[:, qs_], False, True,
                               waits=[mask_ready] + c.cst2, inc=True)
                else:
                    tst = _pe_token(p)
                ps_, pt_, _, pfree = ptr.next()
                if E0 <= 7:
                    e0 = E0 + 3 + q0 // 128
                    ts_, tmp, _, tfree = c.tmpring.next()
                    ta = p.tt(tmp[:, cs_], st[:, cs_], toep[:, e0 * 128:e0 * 128 + (512 - q0)], ALU.add, waits=[tst, tt_, tfree])
                    p.bfree[b] = ta
                    te = p.act(pt_[:, cs_], tmp[:, cs_], AF.Exp, waits=[ta, pfree, pt_free2[ps_]])
                    c.tmpring.free[ts_] = te
                else:
                    te = p.act(pt_[:, cs_], st[:, cs_], AF.Exp, waits=[tst, pfree, pt_free2[ps_]] + c.cst2, bias=c.b31[:, h:h + 1])
                    p.bfree[b] = te
                return (ps_, pt_, te, cs_)

            def back(ti, info, g=g, o=o, tiles=tiles, nt=nt):
                ps_, pt_, te, cs_ = info
                koff = tiles[ti][0]
                lp = p.mm(OT[o][:, cs_], vbuf[s][:, koff // 128, :], pt_[:, cs_], ti == 0, ti == nt - 1,
                          waits=[te, tv, od_free[o]], inc=True)
                ptr.free[ps_] = lp
                if ti == 0:
                    tp_ = p.op("pool", lambda e, pt_=pt_, o=o: e.tensor_copy(out=accP[o][:, :], in_=pt_[:, :]), waits=[te, acc_free[o]], inc=True)
                    acc_tok[o] = tp_
                    pt_free2[ps_] = tp_
                elif ti % 2 == 0:
                    tp_ = p.tt(accP[o][:, cs_], accP[o][:, cs_], pt_[:, cs_], ALU.add, waits=[te, acc_tok[o]], eng="pool")
                    acc_tok[o] = tp_
                    pt_free2[ps_] = tp_
                else:
                    lp = p.mm(DEN[o][:, cs_], c.ones[:, :], pt_[:, cs_], ti == 1, False, waits=[od_free[o]], inc=True)
                    ptr.free[ps_] = lp
                    pt_free2[ps_] = None
                if ti == nt - 1:
                    lp = p.mm(DEN[o][:, :], ones32, accP[o][:, :], False, True, waits=[acc_tok[o], t_ones32], inc=True)
                return lp

            DEPTH = 2
            infos = {}
            for ti in range(nt + DEPTH):
                if ti < nt:
                    infos[ti] = front(ti)
                if ti - DEPTH >= 0:
                    last_pe = back(ti - DEPTH, infos.pop(ti - DEPTH))
            tr_ = p.op("dve", lambda e, o=o: e.reciprocal(out=accP[o][:, :], in_=DEN[o][:, :]), waits=[last_pe], inc=True)
            to = p.tt(A[:, h, g * 512:(g + 1) * 512], OT[o][:, :], accP[o][:, :], ALU.mult, waits=[tr_])
            od_free[o] = to
            acc_free[o] = to
            c.a_last = to
        kfree[s] = last_pe
        toep_free = _dve_token(p)
        mask_free = last_pe
    p.nrot = p.nfb


def _dve_token(p):
    if p.dry:
        return None
    st = p.streams["dve"]
    for ent in reversed(st):
        if ent[1] is not None:
            if ent[2] is None:
                p.cnt["m_dve"] += 1
                ent[2] = ("m_dve", 1)
            break
    return ("m_dve", p.cnt["m_dve"])


def build_l1(p, nc, dr, c=None, hT_in=None, kTa=None, va=None):
    if c is None:
        c = Ctx()
        c.ident_dram = dr["ident"]
        c.out_toks = []
        setup_common(p, nc, c, dr["gains"], 8)
    hT_in = dr["hT_in"] if hT_in is None else hT_in
    kTa = dr["kTa"] if kTa is None else kTa
    va = dr["va"] if va is None else va
    hs = dr["hs"]
    c.b_free = None
    tmpring = c.tmpring
    c.kmb = p.sb("kmb", [128, 16, 16], BF16)
    c.pastb = p.sb("pastb", [128, 16, 16], F32)
    c.b31 = p.sb("b31", [128, 16], F32)
    if hasattr(c, "decT"):
        c.esel = c.decT[:, :, :].rearrange("p a b -> p (a b)").bitcast(BF16).rearrange("p (n k) -> p n k", n=16)
    else:
        c.esel = p.sb("esel", [128, 16, 128], BF16)
    c.kmb_ready = p.dma("pool", c.kmb[:, :, :], dr["kma"], p.sem("cst2p"))
    te = p.dma("pool", c.esel[:, :, :], dr["esel"], p.sem("cst2p"))
    c.cst2 = [p.dma("sync", c.pastb[:, :, :], dr["pastb"], p.sem("cst2")),
              p.dma("sync", c.b31[:, :], dr["b31"], p.sem("cst2")), te, c.kmb_ready]
    c.a_ready = norm_phase(p, c, hT_in, [(GI_ATT, c.A)])
    c.act_ready = c.a_ready
    p.barrier()

    def q_handler(ocg, tg, b, pt, petok):
        t1 = p.act(c.B[:, ocg, tg * 512:(tg + 1) * 512], pt[:, :], AF.Copy, waits=[petok], scale=float(128 ** -0.5))
        p.bfree[b] = t1
        c.q_ready = t1

    proj_fm(p, c, c.A, dr["w_q"], 0, 16, 0, 2048, q_handler)
    p.wfence()
    p.barrier()
    attention_phase(p, c, dr, kTa, va)
    p.barrier()
    p.wopen()
    c.act_ready = c.a_last
    res = Residual(p, c, hT_in, hs)
    proj_fm(p, c, c.A, dr["w_o"], 0, 16, 0, 2048, res.post, pre=res.pre)
    p.barrier()
    c.a_ready = norm_phase(p, c, hs, [(GI_MLP1, c.A)])
    p.barrier()
    mlp_phase(p, c, hs, dr["w_up1"], dr["w_dn1"], tmpring)
    p.barrier()
    pT_sb = c.arena[:, 24576:28672].rearrange("p (k t) -> p k t", k=2)
    c.p_ready = p.dma("pool", pT_sb[:, :, :], dr["pT1"].rearrange("(kc p) t -> p kc t", p=128), p.sem("pld"))
    c.a_ready = norm_phase(p, c, hs, [(GI_PLE1, c.A)])
    ple_phase(p, c, hs, dr["pw_gate1"], dr["pw_up1"], pT_sb, tmpring)
    p.barrier()
    norm_phase(p, c, hs, [(GI_FIN, None)], out_dram=dr["outT"])
    p.barrier()
    p.wait_only("sync", c.out_toks)


def t5_bucket_np(dist):
    n = np.maximum(dist, 0)
    nf = np.maximum(n, 16).astype(np.float32)
    val = (np.log(nf / np.float32(16)) / np.float32(math.log(1024 / 16))).astype(np.float32) * np.float32(16)
    large = 16 + val.astype(np.int32)
    large = np.minimum(large, 31)
    return np.where(n < 16, n, large)


def l1_consts(rel_bias, hf):
    ki = np.arange(128)[:, None, None]
    ei = np.arange(14)[None, :, None]
    qi = np.arange(128)[None, None, :]
    dist = 128 * (ei - 3) + qi - ki
    bucket = t5_bucket_np(dist)
    toep = np.empty((16, 128, 14 * 128), np.float32)
    for h in range(16):
        toep[h] = np.where(dist >= 0, rel_bias[:, h][bucket], np.float32(NEG)).reshape(128, 14 * 128)
    b31 = np.ascontiguousarray(np.broadcast_to(rel_bias[31][None, :], (128, 16))).astype(np.float32)
    esel = np.zeros((128, 16, 128), np.float32)
    for n in range(16):
        esel[n, n, :] = 1.0
    pastb = np.full((128, 16, 16), NEG, np.float32)
    for qt in range(16):
        j = qt // 2
        if hf == 1:
            pastb[:, qt, 0:8] = 0.0
        pastb[:, qt, 8:8 + j] = 0.0
    return toep, b31, esel, pastb


def kernel(**inputs):
    inputs = {k: np.asarray(v) for k, v in inputs.items()}
    x = inputs["x"]
    pp = inputs["p"]
    nc, _ = build_nc_fused()
    zx, decT = ret_consts()
    gains = gain_layout([inputs["ret_norm_g"][0], inputs["mlp_norm_g"][0], inputs["ple_norm_g"][0], inputs["kv_norm_g"],
                         inputs["att_norm_g"][0], inputs["mlp_norm_g"][1], inputs["ple_norm_g"][1], inputs["final_norm_g"]])
    ident = np.eye(128, dtype=np.float32)
    cs = [rot_tables(0), rot_tables(T)]
    ca = np.ascontiguousarray
    shared = dict(gains=gains, ident=ident, zx=zx, decT=decT,
                  w_in=ca(inputs["ret_w_in"][0]), w_out=ca(inputs["ret_w_out"][0]),
                  w_up=ca(inputs["mlp_w_up"][0]), w_dn=ca(inputs["mlp_w_down"][0]),
                  pw_up=ca(inputs["ple_w_up"][0]), pw_gate=ca(inputs["ple_w_gate"][0]), w_kv=ca(inputs["w_kv"]),
                  w_q=ca(inputs["att_w_q"][0]), w_o=ca(inputs["att_w_o"][0]),
                  w_up1=ca(inputs["mlp_w_up"][1]), w_dn1=ca(inputs["mlp_w_down"][1]),
                  pw_up1=ca(inputs["ple_w_up"][1]), pw_gate1=ca(inputs["ple_w_gate"][1]))
    csts = [l1_consts(np.asarray(inputs["rel_bias"], np.float32), hf) for hf in range(2)]
    in_maps = []
    for cid in range(8):
        b, hf = cid // 2, cid % 2
        m = dict(shared)
        toep, b31, esel, pastb = csts[hf]
        m.update(toep=toep, b31=b31, esel=esel, pastb=pastb)
        own = ca(x[b, hf * T:(hf + 1) * T, :].T)
        p_own = ca(pp[0, b, hf * T:(hf + 1) * T, :].T)
        if hf == 1:
            prev = ca(x[b, 0:T, :].T)
            p_prev = ca(pp[0, b, 0:T, :].T)
        else:
            prev = np.zeros_like(own)
            p_prev = np.zeros_like(p_own)
        m["xT2"] = np.stack([prev, own])
        m["pT2"] = np.stack([p_prev, p_own])
        m["cs2"] = np.stack([cs[0], cs[hf]])
        m["pT1"] = ca(pp[1, b, hf * T:(hf + 1) * T, :].T)
        in_maps.append(m)
    res = run_bass_kernel_spmd(nc, in_maps, core_ids=list(range(8)))
    out = np.empty((4, 4096, D), np.float32)
    for cid in range(8):
        b, hf = cid // 2, cid % 2
        out[b, hf * T:(hf + 1) * T, :] = np.asarray(res.results[cid]["outT"]).T
    return out
```
